# Optimizing a Trainium2 kernel written in Bass

```python
import jax, jax.numpy as jnp
from jax import lax
import numpy as np

D_MODEL = 1024
BATCH = 8
SEQ = 2048
DEPTH = 4
DEC_BATCH = 128
DEC_SEQ = 1
PAST_LEN = 16384
PAGE_SIZE = 128

N_MEM = 256
CHUNK = 128
GMLP_GROUPS = 4
GMLP_HALF = D_MODEL // 2
GMLP_GROUP_W = GMLP_HALF // GMLP_GROUPS
CONV_B_W = D_MODEL // 2
CONV_B_K = 31
CONV_C_W = D_MODEL // 2
CONV_C_K = 3
MEM_HEADS = 4
MEM_HEAD_DIM = 128
MEM_W = MEM_HEADS * MEM_HEAD_DIM
D_FF = 2816
N_BRANCH = 4
EPS = 1e-6
OFF_A = 2 * GMLP_HALF
OFF_B = OFF_A + 2 * CONV_B_W
OFF_C = OFF_B + 3 * CONV_C_W
IN_COLS = OFF_C + MEM_W

kernel_name = "hybrid_gmlp_conformer_shortconv_memory_decoder_step"


def rms_norm(x, g):
    xf = x.astype(jnp.float32)
    y = xf * lax.rsqrt(jnp.mean(xf * xf, axis=-1, keepdims=True) + EPS)
    return (y * g.astype(jnp.float32)).astype(x.dtype)


def layer_norm(x, g, b):
    xf = x.astype(jnp.float32)
    mu = jnp.mean(xf, axis=-1, keepdims=True)
    var = jnp.mean(jnp.square(xf - mu), axis=-1, keepdims=True)
    y = (xf - mu) * lax.rsqrt(var + EPS)
    return (y * g.astype(jnp.float32) + b.astype(jnp.float32)).astype(x.dtype)


def swiglu_ffn(h, w_gate_up, w_down):
    g, u = jnp.split(h @ w_gate_up, 2, axis=-1)
    return (jax.nn.silu(g) * u) @ w_down


def chunk_spatial_mix(v, w_s, b_s):
    bsz, t, _ = v.shape
    n_chunks = -(-t // CHUNK)
    vp = jnp.pad(v, ((0, 0), (0, n_chunks * CHUNK - t), (0, 0)))
    vc = vp.reshape(bsz, n_chunks, CHUNK, GMLP_GROUPS, GMLP_GROUP_W)
    mask = jnp.tril(jnp.ones((CHUNK, CHUNK), dtype=bool))
    ws = jnp.where(mask, w_s, 0).astype(v.dtype)
    out = jnp.einsum('gij,bcjgd->bcigd', ws, vc) + jnp.transpose(b_s).astype(v.dtype)[:, :, None]
    return out.reshape(bsz, n_chunks * CHUNK, GMLP_HALF)[:, :t]


def causal_depthwise_conv(x, buf, w):
    k = w.shape[0]
    xc = jnp.concatenate([buf.astype(x.dtype), x], axis=1)
    y = lax.conv_general_dilated(xc, w[:, None, :].astype(x.dtype), window_strides=(1,),
                                 padding='VALID', dimension_numbers=('NWC', 'WIO', 'NWC'),
                                 feature_group_count=x.shape[-1])
    return y, xc[:, -(k - 1):]


def memory_kv(mem, g, w_k, w_v):
    bsz = mem.shape[0]
    m = rms_norm(mem, g)
    k = (m @ w_k).reshape(bsz, N_MEM, MEM_HEADS, MEM_HEAD_DIM)
    v = (m @ w_v).reshape(bsz, N_MEM, MEM_HEADS, MEM_HEAD_DIM)
    return k, v


def memory_attention(q, k, v):
    bsz, t = q.shape[0], q.shape[1]
    s = jnp.einsum('bthd,bmhd->bhtm', q, k.astype(q.dtype)).astype(jnp.float32) * (MEM_HEAD_DIM ** -0.5)
    p = jax.nn.softmax(s, axis=-1).astype(q.dtype)
    o = jnp.einsum('bhtm,bmhd->bthd', p, v.astype(q.dtype))
    return o.reshape(bsz, t, MEM_W)


def trunk_layer(x, mem_k, mem_v, buf_b, buf_c, lp):
    bsz, t, _ = x.shape
    x = x + 0.5 * swiglu_ffn(rms_norm(x, lp['ffn1_norm']), lp['ffn1_w_gate_up'], lp['ffn1_w_down'])
    h = rms_norm(x, lp['mix_norm'])
    z = h @ lp['w_in']
    z_a, z_b, z_c, z_q = z[..., :OFF_A], z[..., OFF_A:OFF_B], z[..., OFF_B:OFF_C], z[..., OFF_C:]
    u, v = jnp.split(jax.nn.gelu(z_a, approximate=False), 2, axis=-1)
    v = layer_norm(v, lp['gmlp_ln_g'], lp['gmlp_ln_b'])
    out_a = (u * chunk_spatial_mix(v, lp['gmlp_w_s'], lp['gmlp_b_s'])) @ lp['gmlp_w_out']
    b_val, b_gate = jnp.split(z_b, 2, axis=-1)
    y_b, new_buf_b = causal_depthwise_conv(b_val * jax.nn.sigmoid(b_gate), buf_b, lp['conv_b_w'])
    y_b = jax.nn.silu(layer_norm(y_b + lp['conv_b_bias'], lp['conv_b_ln_g'], lp['conv_b_ln_b']))
    out_b = y_b @ lp['conv_b_w_out']
    c_gate_b, c_gate_c, c_in = jnp.split(z_c, 3, axis=-1)
    y_c, new_buf_c = causal_depthwise_conv(c_gate_c * c_in, buf_c, lp['conv_c_w'])
    out_c = (c_gate_b * y_c) @ lp['conv_c_w_out']
    q = z_q.reshape(bsz, t, MEM_HEADS, MEM_HEAD_DIM)
    out_m = memory_attention(q, mem_k, mem_v) @ lp['mem_w_out']
    gates = jax.nn.sigmoid(h @ lp['w_branch_gate'] + lp['b_branch_gate']).reshape(bsz, t, N_BRANCH, D_MODEL)
    merged = (gates[:, :, 0] * out_a + gates[:, :, 1] * out_b
              + gates[:, :, 2] * out_c + gates[:, :, 3] * out_m)
    x = x + merged @ lp['w_o']
    x = x + 0.5 * swiglu_ffn(rms_norm(x, lp['ffn2_norm']), lp['ffn2_w_gate_up'], lp['ffn2_w_down'])
    return x, v, new_buf_b, new_buf_c


def setup_inputs(seed: int = 0) -> dict:
    key = jax.random.key(seed)
    ks = iter(jax.random.split(key, 48))
    f32 = jnp.float32
    L, D = DEPTH, D_MODEL

    def normal(shape, scale):
        return jax.random.normal(next(ks), shape, f32) * scale

    def gain(shape):
        return 1.0 + normal(shape, 0.02)

    return {
        'x_prompt': normal((BATCH, SEQ, D), 1.0),
        'x_sample': normal((DEC_BATCH, DEC_SEQ, D), 1.0),
        'mem_prompt': normal((BATCH, N_MEM, D), 1.0),
        'state_conv_b': normal((L, DEC_BATCH, CONV_B_K - 1, CONV_B_W), 0.5),
        'state_conv_c': normal((L, DEC_BATCH, CONV_C_K - 1, CONV_C_W), 0.5),
        'cache_mem_k': normal((L, DEC_BATCH, N_MEM, MEM_HEADS, MEM_HEAD_DIM), 1.0),
        'cache_mem_v': normal((L, DEC_BATCH, N_MEM, MEM_HEADS, MEM_HEAD_DIM), 1.0),
        'ffn1_norm': gain((L, D)),
        'ffn1_w_gate_up': normal((L, D, 2 * D_FF), D ** -0.5),
        'ffn1_w_down': normal((L, D_FF, D), D_FF ** -0.5),
        'mix_norm': gain((L, D)),
        'w_in': normal((L, D, IN_COLS), D ** -0.5),
        'gmlp_ln_g': gain((L, GMLP_HALF)),
        'gmlp_ln_b': normal((L, GMLP_HALF), 0.02),
        'gmlp_w_s': normal((L, GMLP_GROUPS, CHUNK, CHUNK), CHUNK ** -0.5),
        'gmlp_b_s': gain((L, GMLP_GROUPS, CHUNK)),
        'gmlp_w_out': normal((L, GMLP_HALF, D), GMLP_HALF ** -0.5),
        'conv_b_w': normal((L, CONV_B_K, CONV_B_W), CONV_B_K ** -0.5),
        'conv_b_bias': normal((L, CONV_B_W), 0.02),
        'conv_b_ln_g': gain((L, CONV_B_W)),
        'conv_b_ln_b': normal((L, CONV_B_W), 0.02),
        'conv_b_w_out': normal((L, CONV_B_W, D), CONV_B_W ** -0.5),
        'conv_c_w': normal((L, CONV_C_K, CONV_C_W), CONV_C_K ** -0.5),
        'conv_c_w_out': normal((L, CONV_C_W, D), CONV_C_W ** -0.5),
        'mem_norm': gain((L, D)),
        'mem_w_k': normal((L, D, MEM_W), D ** -0.5),
        'mem_w_v': normal((L, D, MEM_W), D ** -0.5),
        'mem_w_out': normal((L, MEM_W, D), MEM_W ** -0.5),
        'w_branch_gate': normal((L, D, N_BRANCH * D), D ** -0.5),
        'b_branch_gate': normal((L, N_BRANCH * D), 0.02),
        'w_o': normal((L, D, D), D ** -0.5),
        'ffn2_norm': gain((L, D)),
        'ffn2_w_gate_up': normal((L, D, 2 * D_FF), D ** -0.5),
        'ffn2_w_down': normal((L, D_FF, D), D_FF ** -0.5),
        'final_norm': gain((D,)),
    }


def reference(x_prompt, x_sample, mem_prompt, state_conv_b, state_conv_c, cache_mem_k, cache_mem_v,
              ffn1_norm, ffn1_w_gate_up, ffn1_w_down, mix_norm, w_in,
              gmlp_ln_g, gmlp_ln_b, gmlp_w_s, gmlp_b_s, gmlp_w_out,
              conv_b_w, conv_b_bias, conv_b_ln_g, conv_b_ln_b, conv_b_w_out,
              conv_c_w, conv_c_w_out, mem_norm, mem_w_k, mem_w_v, mem_w_out,
              w_branch_gate, b_branch_gate, w_o, ffn2_norm, ffn2_w_gate_up, ffn2_w_down, final_norm):
    bp = x_prompt.shape[0]
    zero_buf_b = jnp.zeros((bp, CONV_B_K - 1, CONV_B_W), x_prompt.dtype)
    zero_buf_c = jnp.zeros((bp, CONV_C_K - 1, CONV_C_W), x_prompt.dtype)
    xp, xs = x_prompt, x_sample
    mk_p, mv_p, cb_p, cb_s, cc_p, cc_s, gv_s = [], [], [], [], [], [], []
    for l in range(DEPTH):
        lp = {
            'ffn1_norm': ffn1_norm[l], 'ffn1_w_gate_up': ffn1_w_gate_up[l], 'ffn1_w_down': ffn1_w_down[l],
            'mix_norm': mix_norm[l], 'w_in': w_in[l],
            'gmlp_ln_g': gmlp_ln_g[l], 'gmlp_ln_b': gmlp_ln_b[l], 'gmlp_w_s': gmlp_w_s[l],
            'gmlp_b_s': gmlp_b_s[l], 'gmlp_w_out': gmlp_w_out[l],
            'conv_b_w': conv_b_w[l], 'conv_b_bias': conv_b_bias[l], 'conv_b_ln_g': conv_b_ln_g[l],
            'conv_b_ln_b': conv_b_ln_b[l], 'conv_b_w_out': conv_b_w_out[l],
            'conv_c_w': conv_c_w[l], 'conv_c_w_out': conv_c_w_out[l], 'mem_w_out': mem_w_out[l],
            'w_branch_gate': w_branch_gate[l], 'b_branch_gate': b_branch_gate[l], 'w_o': w_o[l],
            'ffn2_norm': ffn2_norm[l], 'ffn2_w_gate_up': ffn2_w_gate_up[l], 'ffn2_w_down': ffn2_w_down[l],
        }
        k_p, v_p = memory_kv(mem_prompt, mem_norm[l], mem_w_k[l], mem_w_v[l])
        xp, _, nb_p, nc_p = trunk_layer(xp, k_p, v_p, zero_buf_b, zero_buf_c, lp)
        xs, v_s, nb_s, nc_s = trunk_layer(xs, cache_mem_k[l], cache_mem_v[l],
                                          state_conv_b[l], state_conv_c[l], lp)
        mk_p.append(k_p); mv_p.append(v_p)
        cb_p.append(nb_p); cb_s.append(nb_s)
        cc_p.append(nc_p); cc_s.append(nc_s)
        gv_s.append(v_s)
    y_prompt = rms_norm(xp, final_norm)
    y_sample = rms_norm(xs, final_norm)
    return (y_prompt, y_sample, jnp.stack(mk_p), jnp.stack(mv_p), jnp.stack(cb_p), jnp.stack(cb_s),
            jnp.stack(cc_p), jnp.stack(cc_s), jnp.stack(gv_s))
```

```python
import contextlib
import numpy as np
import concourse.bass as bass
import concourse.mybir as mybir
from concourse.bass_utils import run_bass_kernel_spmd

F32 = mybir.dt.float32
BF16 = mybir.dt.bfloat16
AF = mybir.ActivationFunctionType
ALU = mybir.AluOpType
AX = mybir.AxisListType

ENGS = ("pe", "act", "dve", "pool", "sp")
ENG_ATTR = {"pe": "tensor", "act": "scalar", "dve": "vector", "pool": "gpsimd", "sp": "sync"}
EPOCH_LIMIT = 24000

L = 4
D = 1024
KD = 8
DFF = 2816
NFF = 22
TP = 2048
NS = 16
NMEM = 256
EPS = 1e-6
GROUPS = [6, 5, 5]
XT = 768
NSLOT = 6
N_CORES = 8


class Buf:
    __slots__ = ("name", "w", "r")

    def __init__(self, name):
        self.name = name
        self.w = None
        self.r = {}


class _Rec:
    def __init__(self):
        self.calls = []

    def __getattr__(self, name):
        def f(*a, **k):
            self.calls.append((name, a, k))
            return self
        return f


def _record_call(fn):
    r = _Rec()
    fn(r)
    assert len(r.calls) == 1
    return r.calls[0]


class Ctx:
    def __init__(self, nc):
        self.nc = nc
        self.q = {e: [] for e in ENGS}
        self.cnt = {}
        self.epoch = {e: 0 for e in ENGS}
        self.waited = {e: {} for e in ENGS}
        self.semnames = []
        for e in ENGS:
            self._newsem((e, 0))

    def _newsem(self, key):
        self.cnt[key] = 0
        self.semnames.append(key)

    def dmasem(self, name):
        key = ("dma", name)
        if key not in self.cnt:
            self._newsem(key)
        return key

    def _emit_waits(self, eng, deps, rawdeps):
        cur = (eng, self.epoch[eng])
        for (s, v) in sorted(deps, key=str):
            if s == cur:
                if eng == "pe":
                    continue
                if (s, v) not in rawdeps:
                    continue
                if v < self.cnt[cur] - 1:
                    continue
            if self.waited[eng].get(s, 0) >= v:
                continue
            self.q[eng].append(("wait", s, v))
            self.waited[eng][s] = v

    @staticmethod
    def _deps(reads, writes):
        raw = set()
        deps = set()
        for b in reads:
            if b.w is not None:
                raw.add(b.w)
        for b in writes:
            if b.w is not None:
                deps.add(b.w)
            for s, v in b.r.items():
                deps.add((s, v))
        return deps | raw, raw

    @staticmethod
    def _record(tok, reads, writes):
        for b in reads:
            if b.r.get(tok[0], 0) < tok[1]:
                b.r[tok[0]] = tok[1]
        for b in writes:
            b.w = tok
            b.r = {}

    def op(self, eng, fn, reads=(), writes=(), inc=True):
        deps, raw = self._deps(reads, writes)
        self._emit_waits(eng, deps, raw)
        cur = (eng, self.epoch[eng])
        if inc:
            self.cnt[cur] += 1
            tok = (cur, self.cnt[cur])
        else:
            assert eng == "pe"
            tok = (cur, self.cnt[cur] + 1)
        self.q[eng].append(("op", _record_call(fn), cur if inc else None, self.cnt[cur] if inc else 0))
        self._record(tok, reads, writes)
        return tok

    def dma(self, qeng, fn, sem, reads=(), writes=()):
        deps, raw = self._deps(reads, writes)
        self._emit_waits(qeng, deps, raw)
        key = self.dmasem(sem)
        self.cnt[key] += 16
        tok = (key, self.cnt[key])
        self.q[qeng].append(("op", _record_call(fn), key, 16))
        self._record(tok, reads, writes)
        return tok

    def inherit(self, new_bufs, old_bufs):
        deps = {}
        for b in old_bufs:
            if b.w is not None and deps.get(b.w[0], 0) < b.w[1]:
                deps[b.w[0]] = b.w[1]
            for s, v in b.r.items():
                if deps.get(s, 0) < v:
                    deps[s] = v
        for nb in new_bufs:
            for s, v in deps.items():
                if nb.r.get(s, 0) < v:
                    nb.r[s] = v

    def wait_all(self, eng, bufs):
        deps = set()
        for b in bufs:
            if b.w is not None:
                deps.add(b.w)
            for s, v in b.r.items():
                deps.add((s, v))
        self._emit_waits(eng, deps, deps)

    def finalize(self):
        nc = self.nc
        needed = {}
        for eng in ENGS:
            for o in self.q[eng]:
                if o[0] == "wait" and o[1][0] != "dma":
                    needed.setdefault(o[1], set()).add(o[2])
        rank = {k: {v: i + 1 for i, v in enumerate(sorted(vs))} for k, vs in needed.items()}
        self.n_incs = {k: len(v) for k, v in needed.items()}
        with contextlib.ExitStack() as st:
            handles = {}
            for key in self.semnames:
                nm = "s_" + "_".join(str(k) for k in key)
                handles[key] = st.enter_context(nc.semaphore(nm))
            block = st.enter_context(nc.Block())
            for eng in ENGS:
                ops = self.q[eng]

                def run(e, ops=ops):
                    for o in ops:
                        if o[0] == "wait":
                            if o[1][0] == "dma":
                                e.wait_ge(handles[o[1]], o[2])
                            else:
                                e.wait_ge(handles[o[1]], rank[o[1]][o[2]])
                        else:
                            name, a, k = o[1]
                            ins = getattr(e, name)(*a, **k)
                            if o[2] is not None:
                                if o[2][0] == "dma":
                                    ins.then_inc(handles[o[2]], 16)
                                elif o[3] in rank.get(o[2], ()):
                                    ins.then_inc(handles[o[2]], 1)
                getattr(block, ENG_ATTR[eng])(run)


class Rot:
    def __init__(self, items):
        self.items = items
        self.i = 0
        self.held = set()

    def next(self):
        for _ in range(len(self.items) + 1):
            k = self.i
            self.i = (self.i + 1) % len(self.items)
            if k not in self.held:
                return self.items[k]
        raise RuntimeError("all held")

    def hold(self):
        t = self.next()
        self.held.add(self.items.index(t))
        return t

    def release(self, t):
        self.held.discard(self.items.index(t))


def group_info(gi):
    t0 = 128 * sum(GROUPS[:gi])
    npr = 128 * GROUPS[gi]
    ns = NS if gi == len(GROUPS) - 1 else 0
    tiles = []
    c = 0
    while c < npr:
        n = min(512, npr - c)
        tiles.append((c, n, "p"))
        c += n
    mtiles = list(tiles)
    if ns:
        tiles.append((npr, ns, "s"))
        lc0, ln_, _ = mtiles[-1]
        if ln_ + ns <= 512:
            mtiles[-1] = (lc0, ln_ + ns, "p")
        else:
            mtiles.append((npr, ns, "s"))
    return t0, npr, ns, tiles, mtiles


def build_program(stop=None, run_groups=None):
    nc = bass.Bass("TRN2", target_bir_lowering=False)
    c = Ctx(nc)

    def din(name, shape):
        return nc.dram_tensor(name, list(shape), F32, kind="ExternalInput").ap()

    def dout(name, shape):
        return nc.dram_tensor(name, list(shape), F32, kind="ExternalOutput").ap()

    xp = din("xp", [TP, D])
    xs = din("xs", [NS, D])
    mem = din("mem", [NMEM, D])
    scb = din("scb", [L, NS, 30, 512])
    scc = din("scc", [L, NS, 2, 512])
    ck = din("ck", [L, NS, NMEM, 512])
    cv = din("cv", [L, NS, NMEM, 512])
    W = {}
    for name, shape in [
        ("ffn1_norm", [L, D]), ("ffn1_w_gate_up", [L, D, 2 * DFF]), ("ffn1_w_down", [L, DFF, D]),
        ("mix_norm", [L, D]), ("w_in", [L, D, 4096]),
        ("gmlp_ln_g", [L, 512]), ("gmlp_ln_b", [L, 512]), ("gmlp_w_s", [L, 4, 128, 128]),
        ("gmlp_b_s", [L, 4, 128]), ("gmlp_w_out", [L, 512, D]),
        ("conv_b_w", [L, 31, 512]), ("conv_b_bias", [L, 512]), ("conv_b_ln_g", [L, 512]),
        ("conv_b_ln_b", [L, 512]), ("conv_b_w_out", [L, 512, D]),
        ("conv_c_w", [L, 3, 512]), ("conv_c_w_out", [L, 512, D]),
        ("mem_norm", [L, D]), ("mem_w_k", [L, D, 512]), ("mem_w_v", [L, D, 512]),
        ("mem_w_out", [L, 512, D]),
        ("w_branch_gate", [L, D, 4096]), ("b_branch_gate", [L, 4096]), ("w_o", [L, D, D]),
        ("ffn2_norm", [L, D]), ("ffn2_w_gate_up", [L, D, 2 * DFF]), ("ffn2_w_down", [L, DFF, D]),
        ("final_norm", [D]),
    ]:
        W[name] = din(name, shape)
    y_p = dout("y_p", [TP, D])
    y_s = dout("y_s", [NS, D])
    o_mk = dout("o_mk", [L, NMEM, 512])
    o_mv = dout("o_mv", [L, NMEM, 512])
    o_cbp = dout("o_cbp", [L, 30, 512])
    o_cbs = dout("o_cbs", [L, NS, 30, 512])
    o_ccp = dout("o_ccp", [L, 2, 512])
    o_ccs = dout("o_ccs", [L, NS, 2, 512])
    o_gv = dout("o_gv", [L, NS, 512])
    out_bufs = []

    tname = {}

    def sb(name, shape, dt):
        t = nc.alloc_sbuf_tensor(name, list(shape), dt)
        tname[id(t)] = name
        return t, Buf(name)

    def dn(t):
        return "d_" + tname[id(t)]

    x, _bx = sb("x", [128, KD, XT], F32)
    b_x2 = [[Buf(f"x{k}_{ti}") for ti in range(4)] for k in range(KD)]
    h, b_h = sb("h", [128, KD, XT], BF16)
    ring = Rot([sb(f"ring{i}", [128, 4096], BF16) for i in range(NSLOT)])
    ring_id = {id(t[0]): i for i, t in enumerate(ring.items)}
    tf = Rot([sb(f"tf{i}", [128, 512], F32) for i in range(6)])
    tb = Rot([sb(f"tb{i}", [128, 1024], BF16) for i in range(2)])
    sq, b_sq = sb("sq", [128, KD, 512], BF16)
    sq2, _ = sb("sq2", [128, KD, 272], BF16)
    b_sq2 = [Buf("sq2a"), Buf("sq2b")]

    def sqx(ti):
        if ti == 0:
            return sq, b_sq, 0
        if ti == 1:
            return sq2, b_sq2[0], 0
        return sq2, b_sq2[1], 256
    stage = Rot([sb(f"stage{i}", [128, D], F32) for i in range(2)])
    arenaA, _ = sb("arenaA", [128, 8 * XT], BF16)
    act = [(arenaA[:, i * 4 * XT:(i + 1) * 4 * XT].rearrange("p (j t) -> p j t", j=4), Buf(f"act{i}")) for i in range(2)]
    uA, b_uA = sb("uA", [128, 4, XT], BF16)
    regions = {"A": [], "V": [], "X": [], "B": []}

    def claimR(rn, new_bufs):
        c.inherit(new_bufs, regions[rn])
        regions[rn] = list(new_bufs)

    regV, _ = sb("regV", [128, 8 * 512], BF16)
    vtm, b_vtm = regV[:, :].rearrange("p (c d) -> p c d", c=8), Buf("vtm")
    diag_t, b_diag = regV[:, 0:31 * 128].rearrange("p (k c) -> p k c", k=31), Buf("diag")
    b_ybj = [Buf(f"yb{j}") for j in range(4)]
    memn, b_memn = regV[:, 0:KD * NMEM].rearrange("p (k m) -> p k m", k=KD), Buf("memn")
    XCW = 2 + XT
    xc = Rot([(regV[:, i * 1792:i * 1792 + 2 * XCW].bitcast(F32), Buf(f"xc{i}")) for i in range(2)])
    regX, _ = sb("regX", [128, 4 * (30 + XT)], BF16)
    xb, b_xb = regX[:, :].rearrange("p (j t) -> p j t", j=4), Buf("xb")
    gbt = Rot([(regX[:, i * 2 * XT:(i + 1) * 2 * XT].bitcast(F32), Buf(f"gbt{i}")) for i in range(2)])
    regB, _ = sb("regB", [128, 8 * XT], BF16)
    bop, b_bop = regB[:, 0:4 * XT].rearrange("p (j t) -> p j t", j=4), Buf("bop")
    cop, b_cop = regB[:, 4 * XT:8 * XT].rearrange("p (j t) -> p j t", j=4), Buf("cop")
    stfm, b_stfm = regB[:, 0:2 * 4 * NS * 30].bitcast(F32).rearrange("p (j r) -> p j r", j=4), Buf("stfm")
    yb, b_yb = arenaA[:, :].bitcast(F32).rearrange("p (j t) -> p j t", j=4), Buf("yb")
    qm, b_qm = sb("qm", [128, 4, XT], BF16)
    merged, b_merged = arenaA[:, :].rearrange("p (j t) -> p j t", j=KD), Buf("merged")

    def claim(new_bufs):
        claimR("A", new_bufs)
    ident, b_ident = sb("ident", [128, 128], F32)
    ones_b, b_ones = sb("ones_b", [128, 128], BF16)
    ones_row, b_onesrow = sb("ones_row", [1, 128], BF16)
    epsc, b_epsc = sb("epsc", [128, 1], F32)
    idb16, b_idb16 = sb("idb16", [NS, NS], BF16)
    CA, b_CA = sb("CA", [128, L, 128], F32)
    CB, b_CB = sb("CB", [128, L, 128], F32)
    FN, b_FN = sb("FN", [128, 8], F32)
    cstage = Rot([sb(f"cstage{i}", [128, 128], F32) for i in range(2)])
    carryB, b_carryB = sb("carryB", [128, L, 4, 30], BF16)
    carryC, b_carryC = sb("carryC", [128, L, 4, 2], F32)
    kT, b_kT = sb("kT", [128, 4, NMEM], BF16)
    vbf, b_vbf = sb("vbf", [128, 2, 512], BF16)
    wsT, b_wsT = sb("wsT", [128, 4, 128], BF16)
    brow, b_brow = sb("brow", [1, 512], BF16)
    glnbc, b_glnbc = sb("glnbc", [128, 2, 512], F32)
    ws00, b_ws00 = sb("ws00", [128, 8], F32)
    xb32, b_xb32 = sb("xb32", [128, 4, 32], F32)
    xc32, b_xc32 = sb("xc32", [128, 4, 2], F32)
    xbn, b_xbn = sb("xbn", [128, 4, NS], F32)
    xcn, b_xcn = sb("xcn", [128, 4, NS], F32)
    stcfm, b_stcfm = sb("stcfm", [128, 4, NS * 2], F32)
    qtm, b_qtm = sb("qtm", [NS, 512], BF16)
    kst = Rot([sb(f"kst{i}", [128, 2, 512], BF16) for i in range(2)])
    vst = Rot([sb(f"vst{i}", [128, 2, 512], BF16) for i in range(2)])
    prod, b_prod = sb("prod", [128, 2, 512], F32)
    small = Rot([sb(f"small{i}", [128, 64], F32) for i in range(4)])
    esm = Rot([sb(f"esm{i}", [128, 8], BF16) for i in range(2)])
    ostage = Rot([sb(f"ostage{i}", [128, 512], F32) for i in range(2)])
    print("sbuf bytes remaining:", nc.sbuf_bytes_remaining)

    psum = Rot([(nc.alloc_psum_tensor(f"ps{i}", [128, 512], F32), Buf(f"ps{i}")) for i in range(8)])

    def mm(out, lhsT, rhs, start, stop, reads, pbuf, last):
        c.op("pe", lambda e: e.matmul(out, lhsT, rhs, start=start, stop=stop),
             reads=reads, writes=[pbuf], inc=last)

    def tr(out, in_, idn, reads, pbuf, last=True):
        c.op("pe", lambda e: e.transpose(out, in_, idn), reads=list(reads) + [b_ident], writes=[pbuf], inc=last)

    def actf(out, in_, func, reads, writes, bias=None, scale=None):
        kw = {}
        if bias is not None:
            kw["bias"] = bias
        if scale is not None:
            kw["scale"] = scale
        c.op("act", lambda e: e.activation(out=out, in_=in_, func=func, **kw), reads=reads, writes=writes)

    def tt(out, in0, in1, op, reads, writes, eng="dve"):
        c.op(eng, lambda e: e.tensor_tensor(out=out, in0=in0, in1=in1, op=op), reads=reads, writes=writes)

    def ts(out, in0, s1, s2, op0, op1, reads, writes, eng="dve"):
        if s2 is None:
            c.op(eng, lambda e: e.tensor_scalar(out=out, in0=in0, scalar1=s1, scalar2=None, op0=op0),
                 reads=reads, writes=writes)
        else:
            c.op(eng, lambda e: e.tensor_scalar(out=out, in0=in0, scalar1=s1, scalar2=s2, op0=op0, op1=op1),
                 reads=reads, writes=writes)

    def stt(out, in0, scalar, in1, op0, op1, reads, writes, eng="dve"):
        c.op(eng, lambda e: e.scalar_tensor_tensor(out=out, in0=in0, scalar=scalar, in1=in1, op0=op0, op1=op1),
             reads=reads, writes=writes)

    def cp(out, in_, reads, writes, eng="dve"):
        if eng == "act":
            actf(out, in_, AF.Copy, reads, writes)
        else:
            c.op(eng, lambda e: e.tensor_copy(out=out, in_=in_), reads=reads, writes=writes)

    def wload(parts):
        st_, sb_ = ring.next()
        i = ring_id[id(st_)]
        for dst_fn, src in parts:
            dst = dst_fn(st_)
            c.dma("pool", lambda e, dst=dst, src=src: e.dma_start(out=dst, in_=src), f"ring{i}", writes=[sb_])
        return st_, sb_

    def out_dma(dst, src, reads, sem, q="act"):
        ob = Buf("out")
        out_bufs.append(ob)
        c.dma(q, lambda e: e.dma_start(out=dst, in_=src), sem, reads=reads, writes=[ob])

    c.op("pool", lambda e: e.memset(ident[:], 0.0), writes=[b_ident])
    c.op("pool", lambda e: e.affine_select(out=ident[:], in_=ident[:], compare_op=ALU.not_equal, fill=1.0,
                                           base=0, pattern=[[-1, 128]], channel_multiplier=1),
         reads=[b_ident], writes=[b_ident])
    c.op("dve", lambda e: e.memset(ones_b[:], 1.0), writes=[b_ones])
    c.op("dve", lambda e: e.memset(ones_row[:], 1.0), writes=[b_onesrow])
    c.op("dve", lambda e: e.memset(epsc[:], EPS), writes=[b_epsc])
    c.op("dve", lambda e: e.tensor_copy(out=idb16[:], in_=ident[0:NS, 0:NS]), reads=[b_ident], writes=[b_idb16])
    c.op("dve", lambda e: e.memset(carryC[:], 0.0), writes=[b_carryC])
    c.op("dve", lambda e: e.memset(carryB[:], 0.0), writes=[b_carryB])

    def load_cols(rows_list, dst_ap, nrows):
        stg, b_stg = cstage.next()
        c.op("dve", lambda e: e.memset(stg[:], 0.0), writes=[b_stg])
        for (r0, src) in rows_list:
            nr = src.shape[0]
            c.dma("sp", lambda e, r0=r0, nr=nr, src=src: e.dma_start(out=stg[r0:r0 + nr, :], in_=src),
                  dn(stg), writes=[b_stg])
        pt, pb = psum.next()
        tr(pt[:, 0:128], stg[:, :], ident[:], [b_stg], pb)
        cp(dst_ap, pt[:, 0:128], [pb], [b_CA, b_CB, b_FN], eng="act")

    def rows(v, n):
        return v.rearrange("(k p) -> k p", p=128)

    for l in range(L):
        load_cols([
            (0, rows(W["ffn1_norm"][l], 8)), (8, rows(W["mix_norm"][l], 8)),
            (16, rows(W["ffn2_norm"][l], 8)), (24, rows(W["mem_norm"][l], 8)),
            (32, rows(W["b_branch_gate"][l], 32)),
            (64, rows(W["conv_b_bias"][l], 4)), (68, rows(W["conv_b_ln_g"][l], 4)),
            (72, rows(W["conv_b_ln_b"][l], 4)),
            (76, W["conv_c_w"][l].rearrange("k (j p) -> (k j) p", p=128)),
        ], CA[:, l, :], 88)
        load_cols([(0, W["conv_b_w"][l].rearrange("k (j p) -> (k j) p", p=128))], CB[:, l, :], 124)
    stg, b_stg = cstage.next()
    c.op("dve", lambda e: e.memset(stg[:], 0.0), writes=[b_stg])
    c.dma("sp", lambda e: e.dma_start(out=stg[0:8, :], in_=rows(W["final_norm"], 8)), dn(stg), writes=[b_stg])
    pt, pb = psum.next()
    tr(pt[:, 0:128], stg[:, :], ident[:], [b_stg], pb)
    cp(FN[:, :], pt[:, 0:8], [pb], [b_FN], eng="act")

    COL = {"ffn1_norm": 0, "mix_norm": 8, "ffn2_norm": 16, "mem_norm": 24, "bg": 32,
           "cb_bias": 64, "cb_ln_g": 68, "cb_ln_b": 72, "ccw": 76}

    def colA(l, name, j):
        o = COL[name] + j
        return CA[:, l, o:o + 1]

    def rmsnorm(tiles, gcol_fn, out_fn, out_bufs_w):
        order = sorted(range(len(tiles)), key=lambda i: (tiles[i][1], i))
        for ti in order:
            c0, n, _ = tiles[ti]
            for k in range(KD):
                actf(sq[:, k, 0:n], x[:, k, c0:c0 + n], AF.Square, [b_x2[k][ti]], [b_sq])
            pt, pb = psum.next()
            for k in range(KD):
                mm(pt[:, 0:n], ones_b[:, :], sq[:, k, 0:n], k == 0, k == KD - 1, [b_ones, b_sq], pb, k == KD - 1)
            r_t, r_b = tf.next()
            actf(r_t[:, 0:n], pt[:, 0:n], AF.Sqrt, [pb, b_epsc], [r_b], bias=epsc[:, 0:1], scale=1.0 / D)
            c.op("dve", lambda e, r_t=r_t, n=n: e.reciprocal(out=r_t[:, 0:n], in_=r_t[:, 0:n]), reads=[r_b], writes=[r_b])
            for k in range(KD):
                stt(out_fn(k, c0, n), x[:, k, c0:c0 + n], gcol_fn(k), r_t[:, 0:n], ALU.mult, ALU.mult,
                    [b_x2[k][ti], r_b, b_CA, b_FN], out_bufs_w)

    def ffn(l, tiles, wgu, wdn, normname, hooks=None):
        claim([act[0][1], act[1][1]])
        if hooks and len(hooks) > 2 and hooks[2]:
            hooks[2]()
        rmsnorm(tiles, lambda k: colA(l, normname, k), lambda k, c0, n: h[:, k, c0:c0 + n], [b_h])
        if hooks and hooks[0]:
            hooks[0]()
        ffg = [(i, min(4, NFF - i)) for i in range(0, NFF, 4)]
        wv = wgu[l].rearrange("(k p) c -> p k c", p=128)

        def up(gi_, f0, G):
            sg_, bg_ = wload([(lambda t: t[:, 0:8 * G * 128].rearrange("p (k c) -> p k c", k=8),
                               wv[:, :, f0 * 128:(f0 + G) * 128])])
            su_, bu_ = wload([(lambda t: t[:, 0:8 * G * 128].rearrange("p (k c) -> p k c", k=8),
                               wv[:, :, DFF + f0 * 128:DFF + (f0 + G) * 128])])
            sgv = sg_[:, 0:8 * G * 128].rearrange("p (k c) -> p k c", k=8)
            suv = su_[:, 0:8 * G * 128].rearrange("p (k c) -> p k c", k=8)
            a_t, a_b = act[gi_ % 2]
            for (c0, n, _) in sorted(tiles, key=lambda t_: t_[1]):
                for j in range(G):
                    pg, pgb = psum.next()
                    pu, pub = psum.next()
                    for k in range(KD):
                        mm(pg[:, 0:n], sgv[:, k, j * 128:(j + 1) * 128], h[:, k, c0:c0 + n], k == 0, k == KD - 1,
                           [bg_, b_h], pgb, k == KD - 1)
                    for k in range(KD):
                        mm(pu[:, 0:n], suv[:, k, j * 128:(j + 1) * 128], h[:, k, c0:c0 + n], k == 0, k == KD - 1,
                           [bu_, b_h], pub, k == KD - 1)
                    s_t, s_b = tf.next()
                    actf(s_t[:, 0:n], pg[:, 0:n], AF.Silu, [pgb], [s_b])
                    tt(a_t[:, j, c0:c0 + n], s_t[:, 0:n], pu[:, 0:n], ALU.mult, [s_b, pub], [a_b])

        def down(gi_, f0, G):
            sd_, bd_ = wload([(lambda t: t[:, 0:G * 1024].rearrange("p (j c) -> p j c", j=G),
                               wdn[l][f0 * 128:(f0 + G) * 128, :].rearrange("(j p) c -> p j c", p=128))])
            sdv = sd_[:, 0:G * 1024].rearrange("p (j c) -> p j c", j=G)
            a_t, a_b = act[gi_ % 2]
            for ti, (c0, n, _) in enumerate(tiles):
                for dm in range(KD):
                    pt, pb = psum.next()
                    for j in range(G):
                        mm(pt[:, 0:n], sdv[:, j, dm * 128:(dm + 1) * 128], a_t[:, j, c0:c0 + n], j == 0, j == G - 1,
                           [bd_, a_b], pb, j == G - 1)
                    stt(x[:, dm, c0:c0 + n], pt[:, 0:n], 0.5, x[:, dm, c0:c0 + n], ALU.mult, ALU.add,
                        [pb, b_x2[dm][ti]], [b_x2[dm][ti]])

        up(0, *ffg[0])
        for i in range(len(ffg)):
            if i + 1 < len(ffg):
                up(i + 1, *ffg[i + 1])
            if hooks and hooks[1] and i == 0:
                hooks[1]()
            down(i, *ffg[i])

    kvst = {}

    def kv_a(l, gi):
        first_group = gi == 0
        kvst["stg"] = []
        for mc in range(2):
            stg, b_stg = stage.next()
            c.dma("sp", lambda e, stg=stg, mc=mc: e.dma_start(out=stg[:, :], in_=mem[mc * 128:(mc + 1) * 128, :]),
                  dn(stg), writes=[b_stg])
            sm_t, sm_b = small.next()
            c.op("dve", lambda e, sm_t=sm_t: e.memset(sm_t[:, 0:8], 0.0), writes=[sm_b])
            j_t, j_b = tf.next()
            for hh in range(2):
                c.op("act", lambda e, stg=stg, sm_t=sm_t, j_t=j_t, hh=hh: e.activation(
                    out=j_t[:, :], in_=stg[:, hh * 512:(hh + 1) * 512], func=AF.Square, accum_out=sm_t[:, hh:hh + 1]),
                    reads=[b_stg], writes=[j_b, sm_b])
            tt(sm_t[:, 2:3], sm_t[:, 0:1], sm_t[:, 1:2], ALU.add, [sm_b], [sm_b])
            actf(sm_t[:, 3:4], sm_t[:, 2:3], AF.Sqrt, [sm_b, b_epsc], [sm_b], bias=epsc[:, 0:1], scale=1.0 / D)
            c.op("dve", lambda e, sm_t=sm_t: e.reciprocal(out=sm_t[:, 4:5], in_=sm_t[:, 3:4]), reads=[sm_b], writes=[sm_b])
            ts(stg[:, :], stg[:, :], sm_t[:, 4:5], None, ALU.mult, None, [b_stg, sm_b], [b_stg])
            kvst["stg"].append((stg, b_stg))

    def kv_b1(l, gi):
        claimR("V", [b_memn])
        for mc in range(2):
            stg, b_stg = kvst["stg"][mc]
            for k4 in range(2):
                pt, pb = psum.next()
                for kk in range(4):
                    k = k4 * 4 + kk
                    tr(pt[:, kk * 128:(kk + 1) * 128], stg[:, k * 128:(k + 1) * 128], ident[:], [b_stg], pb, last=(kk == 3))
                for kk in range(4):
                    k = k4 * 4 + kk
                    ts(memn[:, k, mc * 128:(mc + 1) * 128], pt[:, kk * 128:(kk + 1) * 128], colA(l, "mem_norm", k), None,
                       ALU.mult, None, [pb, b_CA], [b_memn])

    def kv_b(l, gi):
        first_group = gi == 0
        slot_view = lambda t: t[:, 0:4096].rearrange("p (k c) -> p k c", k=8)
        wk_t, wk_b = wload([(slot_view, W["mem_w_k"][l].rearrange("(k p) c -> p k c", p=128))])
        wv_t, wv_b = wload([(slot_view, W["mem_w_v"][l].rearrange("(k p) c -> p k c", p=128))])
        wkv = slot_view(wk_t)
        wvv = slot_view(wv_t)
        for mc in range(2):
            for (wt, wb_, o_dram, is_v) in ((wkv, wk_b, o_mk, False), (wvv, wv_b, o_mv, True)):
                pt, pb = psum.next()
                for k in range(KD):
                    mm(pt[:, :], memn[:, k, mc * 128:(mc + 1) * 128], wt[:, k, :], k == 0, k == KD - 1, [b_memn, wb_], pb, k == KD - 1)
                if first_group:
                    o_t, o_b = ostage.next()
                    cp(o_t[:, :], pt[:, :], [pb], [o_b, pb], eng="act")
                    out_dma(o_dram[l, mc * 128:(mc + 1) * 128, :], o_t[:, :], [o_b], dn(o_t), q="act")
                if is_v:
                    cp(vbf[:, mc, :], pt[:, :], [pb], [b_vbf, pb], eng="dve")
        for hd in range(4):
            pt, pb = psum.next()
            for k in range(KD):
                mm(pt[:, 0:NMEM], wkv[:, k, hd * 128:(hd + 1) * 128], memn[:, k, :], k == 0, k == KD - 1, [wk_b, b_memn], pb, k == KD - 1)
            cp(kT[:, hd, :], pt[:, 0:NMEM], [pb], [b_kT], eng="act")


    def mixing(l, gi, t0, npr, ns, tiles, mtiles, mstop=None):
        last_group = gi == len(GROUPS) - 1
        first_group = gi == 0
        ptiles = [t for t in tiles if t[2] == "p"]
        stile = [t for t in tiles if t[2] == "s"]
        nch = npr // 128
        rmsnorm(mtiles, lambda k: colA(l, "mix_norm", k), lambda k, c0, n: h[:, k, c0:c0 + n], [b_h])

        if mstop == "kv":
            return
        wsr_t, b_wsraw = stage.next()
        wsraw = wsr_t[:, 0:512].rearrange("p (g j) -> p g j", g=4)
        c.dma("sp", lambda e: e.dma_start(out=wsraw, in_=W["gmlp_w_s"][l].rearrange("g i j -> i g j")),
              dn(wsr_t), writes=[b_wsraw])
        for g in range(4):
            c.op("pool", lambda e, g=g: e.affine_select(out=wsraw[:, g, :], in_=wsraw[:, g, :], compare_op=ALU.is_ge,
                                                        fill=0.0, base=0, pattern=[[-1, 128]], channel_multiplier=1),
                 reads=[b_wsraw], writes=[b_wsraw])
        pt, pb = psum.next()
        for g in range(4):
            tr(pt[:, g * 128:(g + 1) * 128], wsraw[:, g, :], ident[:], [b_wsraw], pb, last=(g == 3))
        cp(wsT[:, :, :], pt[:, :].rearrange("p (g i) -> p g i", g=4), [pb], [b_wsT], eng="act")
        c.dma("pool", lambda e: e.dma_start(out=brow[:, :], in_=W["gmlp_b_s"][l].rearrange("g i -> (g i)").rearrange("(o n) -> o n", o=1)),
              "brow", writes=[b_brow])
        c.dma("act", lambda e: e.dma_start(out=glnbc[:, 0, :], in_=W["gmlp_ln_g"][l:l + 1, :].partition_broadcast(128)),
              "glnbc", writes=[b_glnbc])
        c.dma("act", lambda e: e.dma_start(out=glnbc[:, 1, :], in_=W["gmlp_ln_b"][l:l + 1, :].partition_broadcast(128)),
              "glnbc", writes=[b_glnbc])
        if ns:
            for g in range(4):
                c.dma("act", lambda e, g=g: e.dma_start(out=ws00[:, g:g + 1], in_=W["gmlp_w_s"][l, g, 0:1, 0:1].partition_broadcast(128)),
                      "ws00", writes=[b_ws00])
                c.dma("act", lambda e, g=g: e.dma_start(out=ws00[:, 4 + g:5 + g], in_=W["gmlp_b_s"][l, g:g + 1, 0:1].partition_broadcast(128)),
                      "ws00", writes=[b_ws00])

        slot_view = lambda t: t[:, 0:4096].rearrange("p (k c) -> p k c", k=8)
        win = W["w_in"][l].rearrange("(k p) c -> p k c", p=128)

        def load_in(blk):
            return wload([(slot_view, win[:, :, blk * 512:(blk + 1) * 512])])

        def fm_block(wt_v, wb_, j, c0, n):
            pt, pb = psum.next()
            for k in range(KD):
                mm(pt[:, 0:n], wt_v[:, k, j * 128:(j + 1) * 128], h[:, k, c0:c0 + n], k == 0, k == KD - 1, [wb_, b_h], pb, k == KD - 1)
            return pt, pb

        w0_t, w0_b = load_in(0)
        w0v = slot_view(w0_t)
        for (c0, n, _) in sorted(mtiles, key=lambda t_: t_[1]):
            for j in range(4):
                pt, pb = fm_block(w0v, w0_b, j, c0, n)
                actf(uA[:, j, c0:c0 + n], pt[:, 0:n], AF.Gelu, [pb], [b_uA])
        w1_t, w1_b = load_in(1)
        w1v = slot_view(w1_t)
        claimR("V", [b_vtm])
        vchunks = [(ci * 128, 128, ci) for ci in range(nch)] + ([(npr, ns, nch)] if ns else [])
        for (c0, n, ci) in vchunks:
            pt, pb = psum.next()
            for k in range(KD):
                mm(pt[0:n, :], h[:, k, c0:c0 + n], w1v[:, k, :], k == 0, k == KD - 1, [b_h, w1_b], pb, k == KD - 1)
            g_t, g_b = tf.next()
            actf(g_t[0:n, :], pt[0:n, :], AF.Gelu, [pb], [g_b])
            sm_t, sm_b = small.next()
            c.op("dve", lambda e, g_t=g_t, sm_t=sm_t, n=n: e.bn_stats(out=sm_t[0:n, 0:6], in_=g_t[0:n, :]), reads=[g_b], writes=[sm_b])
            c.op("dve", lambda e, sm_t=sm_t, n=n: e.bn_aggr(out=sm_t[0:n, 8:10], in_=sm_t[0:n, 0:6]), reads=[sm_b], writes=[sm_b])
            actf(sm_t[0:n, 10:11], sm_t[0:n, 9:10], AF.Sqrt, [sm_b, b_epsc], [sm_b], bias=epsc[0:n, 0:1], scale=1.0)
            c.op("dve", lambda e, sm_t=sm_t, n=n: e.reciprocal(out=sm_t[0:n, 11:12], in_=sm_t[0:n, 10:11]), reads=[sm_b], writes=[sm_b])
            stt(sm_t[0:n, 12:13], sm_t[0:n, 8:9], -1.0, sm_t[0:n, 11:12], ALU.mult, ALU.mult, [sm_b], [sm_b])
            ts(g_t[0:n, :], g_t[0:n, :], sm_t[0:n, 11:12], sm_t[0:n, 12:13], ALU.mult, ALU.add, [g_b, sm_b], [g_b])
            tt(g_t[0:n, :], g_t[0:n, :], glnbc[0:n, 0, :], ALU.mult, [g_b, b_glnbc], [g_b])
            if ci < nch:
                tt(vtm[0:n, ci, :], g_t[0:n, :], glnbc[0:n, 1, :], ALU.add, [g_b, b_glnbc], [b_vtm])
            else:
                tt(g_t[0:n, :], g_t[0:n, :], glnbc[0:n, 1, :], ALU.add, [g_b, b_glnbc], [g_b])
                out_dma(o_gv[l, :, :], g_t[0:n, :], [g_b], dn(g_t))
                cp(vtm[0:NS, nch, :], g_t[0:NS, :], [g_b], [b_vtm], eng="dve")
        if mstop == "A":
            return
        wq_t, wq_b = load_in(7)
        wqv = slot_view(wq_t)
        qscale = 128 ** -0.5
        for hd in range(4):
            for (c0, n, _) in mtiles:
                pt, pb = fm_block(wqv, wq_b, hd, c0, n)
                actf(qm[:, hd, c0:c0 + n], pt[:, 0:n], AF.Identity, [pb], [b_qm], scale=qscale)
        if ns:
            pt, pb = psum.next()
            for k in range(KD):
                mm(pt[0:NS, :], h[:, k, npr:npr + NS], wqv[:, k, :], k == 0, k == KD - 1, [b_h, wq_b], pb, k == KD - 1)
            actf(qtm[:, :], pt[0:NS, :], AF.Identity, [pb], [b_qtm], scale=qscale)
        steps = []
        sacc = {}

        def pump():
            if steps:
                steps.pop(0)()

        if ns:
            sacc["p"] = psum.hold()
            pend = {}

            def step_a(s_):
                pacc, paccb = sacc["p"]
                k_t, k_b = kst.next()
                v_t, v_b = vst.next()
                c.dma("pool", lambda e: e.dma_start(out=k_t[:, :, :], in_=ck[l, s_].rearrange("(mc p) c -> p mc c", p=128)),
                      dn(k_t), writes=[k_b])
                c.dma("pool", lambda e: e.dma_start(out=v_t[:, :, :], in_=cv[l, s_].rearrange("(mc p) c -> p mc c", p=128)),
                      dn(v_t), writes=[v_b])
                pq, pqb = psum.next()
                mm(pq[:, :], idb16[:, s_:s_ + 1].broadcast_to([NS, 128]), qtm[:, :], True, True, [b_idb16, b_qtm], pqb, True)
                tt(prod[:, :, :], k_t[:, :, :], pq[:, None, :].broadcast_to([128, 2, 512]), ALU.mult, [k_b, pqb], [b_prod])
                sm_t, sm_b = small.next()
                c.op("dve", lambda e: e.tensor_reduce(out=sm_t[:, 0:8], in_=prod[:, :, :].rearrange("p m (h d) -> p (m h) d", h=4),
                                                      axis=AX.X, op=ALU.add), reads=[b_prod], writes=[sm_b])
                e_t, e_b = esm.next()
                actf(e_t[:, 0:8], sm_t[:, 0:8], AF.Exp, [sm_b], [e_b])
                pend[s_] = (v_t, v_b, e_t, e_b)

            def step_b(s_):
                pacc, paccb = sacc["p"]
                v_t, v_b, e_t, e_b = pend.pop(s_)
                for hd in range(4):
                    for mc in range(2):
                        mm(pacc[:, hd * NS + s_:hd * NS + s_ + 1], v_t[:, mc, hd * 128:(hd + 1) * 128], e_t[:, mc * 4 + hd:mc * 4 + hd + 1],
                           mc == 0, mc == 1, [v_b, e_b], paccb, False)
                for mc in range(2):
                    mm(pacc[:, 64 + s_ * 4:64 + s_ * 4 + 4], ones_b[:, :], e_t[:, mc * 4:mc * 4 + 4], mc == 0, mc == 1, [b_ones, e_b], paccb, mc == 1)

            steps.append(lambda: step_a(0))
            for s_ in range(NS):
                if s_ + 1 < NS:
                    steps.append(lambda s_=s_: step_a(s_ + 1))
                steps.append(lambda s_=s_: step_b(s_))

        claim(b_ybj)
        claimR("X", [b_xb])
        if ns:
            claimR("B", [b_stfm])
        wv_t2, wv_b2 = load_in(2)
        wg_t2, wg_b2 = load_in(3)
        wvalv = slot_view(wv_t2)
        wgatv = slot_view(wg_t2)
        if first_group:
            c.op("dve", lambda e: e.memset(xb[:, :, 0:30], 0.0), writes=[b_xb])
        else:
            cp(xb[:, :, 0:30], carryB[:, l, :, :], [b_carryB], [b_xb], eng="dve")
        if ns:
            for i4 in range(4):
                stg, b_stg = stage.next()
                c.dma("sp", lambda e, stg=stg, i4=i4: e.dma_start(
                    out=stg[0:120, 0:512], in_=scb[l, i4 * 4:(i4 + 1) * 4, :, :].rearrange("s k c -> (s k) c")),
                    dn(stg), writes=[b_stg])
                pt, pb = psum.next()
                for j in range(4):
                    tr(pt[:, j * 120:(j + 1) * 120], stg[0:120, j * 128:(j + 1) * 128], ident[0:120, 0:120], [b_stg], pb, last=(j == 3))
                cp(stfm[:, :, i4 * 120:(i4 + 1) * 120], pt[:, 0:480].rearrange("p (j r) -> p j r", j=4), [pb], [b_stfm], eng="act")
            ob = Buf("out")
            out_bufs.append(ob)
            c.dma("sp", lambda e: e.dma_start(out=o_cbs[l, :, 0:29, :], in_=scb[l, :, 1:30, :]), "d2d", writes=[ob])
        for j in range(4):
            for (c0, n, kind) in tiles:
                pv_, pvb_ = fm_block(wvalv, wv_b2, j, c0, n)
                pg_, pgb_ = fm_block(wgatv, wg_b2, j, c0, n)
                pump()
                s_t, s_b = tf.next()
                actf(s_t[:, 0:n], pg_[:, 0:n], AF.Sigmoid, [pgb_], [s_b])
                if kind == "p":
                    tt(xb[:, j, 30 + c0:30 + c0 + n], pv_[:, 0:n], s_t[:, 0:n], ALU.mult, [pvb_, s_b], [b_xb])
                    if last_group and c0 + n == npr:
                        tt(xb32[:, j, 0:30], pv_[:, n - 30:n], s_t[:, n - 30:n], ALU.mult, [pvb_, s_b], [b_xb32])
                else:
                    tt(xbn[:, j, :], pv_[:, 0:n], s_t[:, 0:n], ALU.mult, [pvb_, s_b], [b_xbn])
        if not last_group:
            cp(carryB[:, l, :, :], xb[:, :, npr:npr + 30], [b_xb], [b_carryB], eng="dve")
        if ns:
            pv, pvb = psum.next()
            for g in range(4):
                mm(pv[:, g * NS:(g + 1) * NS], vtm[0:NS, nch, g * 128:(g + 1) * 128], idb16[:, :], True, True, [b_vtm, b_idb16], pvb, g == 3)
            m_t, m_b = small.next()
            for g in range(4):
                ts(m_t[:, g * NS:(g + 1) * NS], pv[:, g * NS:(g + 1) * NS], ws00[:, g:g + 1], ws00[:, 4 + g:5 + g],
                   ALU.mult, ALU.add, [pvb, b_ws00], [m_b])
            tt(uA[:, :, npr:npr + NS], uA[:, :, npr:npr + NS], m_t[:, 0:4 * NS].rearrange("p (g s) -> p g s", g=4),
               ALU.mult, [b_uA, m_b], [b_uA])
        for ci in range(nch):
            pt, pb = psum.next()
            for g in range(4):
                mm(pt[:, g * 128:(g + 1) * 128], vtm[:, ci, g * 128:(g + 1) * 128], wsT[:, g, :], True, False, [b_vtm, b_wsT], pb, False)
                mm(pt[:, g * 128:(g + 1) * 128], ones_row[0:1, :], brow[0:1, g * 128:(g + 1) * 128], False, True, [b_onesrow, b_brow], pb, g == 3)
            tt(uA[:, :, ci * 128:(ci + 1) * 128], uA[:, :, ci * 128:(ci + 1) * 128],
               pt[:, :].rearrange("p (g i) -> p g i", g=4), ALU.mult, [b_uA, pb], [b_uA])

        def ln_prep(j, ti, c0, n, off=0):
            q_t, q_b, o = sqx(ti)
            o += off
            cp(q_t[:, j, o:o + n], yb[:, j, c0:c0 + n], [b_ybj[j]], [q_b], eng="dve")
            actf(q_t[:, 4 + j, o:o + n], yb[:, j, c0:c0 + n], AF.Square, [b_ybj[j]], [q_b])

        asteps = []

        def pump_att(k_):
            for _ in range(k_):
                if asteps:
                    asteps.pop(0)()

        att_items = [(hd, c0, n) for hd in range(4) for (c0, n, kind) in ptiles]
        att_state = {}

        def att_qk(i):
            hd, c0, n = att_items[i]
            e_t, e_b = tb.next()
            for mc in range(2):
                pt, pb = psum.next()
                mm(pt[:, 0:n], kT[:, hd, mc * 128:(mc + 1) * 128], qm[:, hd, c0:c0 + n], True, True, [b_kT, b_qm], pb, True)
                actf(e_t[:, mc * 512:mc * 512 + n], pt[:, 0:n], AF.Exp, [pb], [e_b])
            att_state[i] = (e_t, e_b)

        def att_pv(i):
            hd, c0, n = att_items[i]
            e_t, e_b = att_state.pop(i)
            po, pob = psum.next()
            pd, pdb = psum.next()
            for mc in range(2):
                mm(po[:, 0:n], vbf[:, mc, hd * 128:(hd + 1) * 128], e_t[:, mc * 512:mc * 512 + n], mc == 0, mc == 1, [b_vbf, e_b], pob, mc == 1)
            for mc in range(2):
                mm(pd[:, 0:n], ones_b[:, :], e_t[:, mc * 512:mc * 512 + n], mc == 0, mc == 1, [b_ones, e_b], pdb, mc == 1)
            r_t, r_b = tf.next()
            c.op("dve", lambda e: e.reciprocal(out=r_t[:, 0:n], in_=pd[:, 0:n]), reads=[pdb], writes=[r_b])
            tt(qm[:, hd, c0:c0 + n], po[:, 0:n], r_t[:, 0:n], ALU.mult, [pob, r_b], [b_qm])

        asteps.append(lambda: att_qk(0))
        for i in range(len(att_items)):
            if i + 1 < len(att_items):
                asteps.append(lambda i=i: att_qk(i + 1))
            asteps.append(lambda i=i: att_pv(i))

        claimR("V", [b_diag])
        wv3 = CB[:, l, :].rearrange("p (k j) -> p j k", j=4)
        for j in range(4):
            tt(diag_t[:, :, :], ident[:, :].unsqueeze(1).broadcast_to([128, 31, 128]),
               wv3[:, j, 0:31].unsqueeze(2).broadcast_to([128, 31, 128]), ALU.mult, [b_ident, b_CB], [b_diag])
            for ti, (c0, n, kind) in enumerate(ptiles):
                pump_att(1)
                pt, pb = psum.next()
                for k in range(31):
                    mm(pt[:, 0:n], diag_t[:, k, :], xb[:, j, c0 + k:c0 + k + n], k == 0, k == 30, [b_diag, b_xb], pb, k == 30)
                actf(yb[:, j, c0:c0 + n], pt[:, 0:n], AF.Identity, [pb, b_CA], [b_ybj[j]], bias=colA(l, "cb_bias", j), scale=1.0)
                ln_prep(j, ti, c0, n)
                pump_att(1)
            if ns:
                tt(prod[:, :, :].rearrange("p a b -> p (a b)")[:, 0:NS * 30].rearrange("p (s k) -> p s k", k=30), stfm[:, j, :].rearrange("p (s k) -> p s k", k=30),
                   wv3[:, j:j + 1, 0:30].broadcast_to([128, NS, 30]), ALU.mult, [b_stfm, b_CB], [b_prod])
                sm_t, sm_b = small.next()
                c.op("dve", lambda e, sm_t=sm_t: e.tensor_reduce(out=sm_t[:, 0:NS], in_=prod[:, :, :].rearrange("p a b -> p (a b)")[:, 0:NS * 30].rearrange("p (s k) -> p s k", k=30),
                                                                 axis=AX.X, op=ALU.add), reads=[b_prod], writes=[sm_b])
                stt(sm_t[:, 0:NS], xbn[:, j, :], CB[:, l, 30 * 4 + j:30 * 4 + j + 1], sm_t[:, 0:NS], ALU.mult, ALU.add,
                    [b_xbn, b_CB, sm_b], [sm_b])
                ts(yb[:, j, npr:npr + NS], sm_t[:, 0:NS], colA(l, "cb_bias", j), None, ALU.add, None, [sm_b, b_CA], [b_ybj[j]])
                ln_prep(j, len(ptiles) - 1, npr, NS, off=npr - ptiles[-1][0])
        if last_group:
            pt, pb = psum.next()
            for j in range(4):
                tr(pt[0:30, j * 128:(j + 1) * 128], xb32[:, j, 0:30], ident[:, :], [b_xb32], pb, last=(j == 3))
            o_t, o_b = ostage.next()
            cp(o_t[0:30, :], pt[0:30, :], [pb], [o_b], eng="act")
            out_dma(o_cbp[l, :, :], o_t[0:30, :], [o_b], dn(o_t))
            pt, pb = psum.next()
            for j in range(4):
                tr(pt[0:NS, j * 128:(j + 1) * 128], xbn[:, j, :], ident[:, :], [b_xbn], pb, last=(j == 3))
            o_t, o_b = ostage.next()
            cp(o_t[0:NS, :], pt[0:NS, :], [pb], [o_b], eng="act")
            out_dma(o_cbs[l, :, 29, :], o_t[0:NS, :], [o_b], dn(o_t))
        if mstop == "B":
            return
        def ln_block():
            for ti, (c0, n, _) in enumerate(mtiles):
                q_t, q_b, o = sqx(ti)
                p1, p1b = psum.next()
                p2, p2b = psum.next()
                for j in range(4):
                    mm(p1[:, 0:n], ones_b[:, :], q_t[:, j, o:o + n], j == 0, j == 3, [b_ones, q_b], p1b, j == 3)
                for j in range(4):
                    mm(p2[:, 0:n], ones_b[:, :], q_t[:, 4 + j, o:o + n], j == 0, j == 3, [b_ones, q_b], p2b, j == 3)
                mean_t, mean_b = tf.next()
                var_t, var_b = tf.next()
                cp(mean_t[:, 0:n], p1[:, 0:n], [p1b], [mean_b], eng="act")
                stt(var_t[:, 0:n], mean_t[:, 0:n], 1.0 / 512, mean_t[:, 0:n], ALU.mult, ALU.mult, [mean_b], [var_b])
                tt(var_t[:, 0:n], p2[:, 0:n], var_t[:, 0:n], ALU.subtract, [p2b, var_b], [var_b])
                actf(var_t[:, 0:n], var_t[:, 0:n], AF.Sqrt, [var_b, b_epsc], [var_b], bias=epsc[:, 0:1], scale=1.0 / 512)
                c.op("dve", lambda e, var_t=var_t, n=n: e.reciprocal(out=var_t[:, 0:n], in_=var_t[:, 0:n]), reads=[var_b], writes=[var_b])
                for j in range(4):
                    t_t, t_b = tf.next()
                    stt(t_t[:, 0:n], mean_t[:, 0:n], -1.0 / 512, yb[:, j, c0:c0 + n], ALU.mult, ALU.add, [mean_b, b_ybj[j]], [t_b])
                    tt(t_t[:, 0:n], t_t[:, 0:n], var_t[:, 0:n], ALU.mult, [t_b, var_b], [t_b])
                    actf(bop[:, j, c0:c0 + n], t_t[:, 0:n], AF.Silu, [t_b, b_CA], [b_bop],
                         bias=colA(l, "cb_ln_b", j), scale=colA(l, "cb_ln_g", j))


        claimR("B", [b_bop, b_cop])
        claimR("V", [t[1] for t in xc.items])
        claimR("X", [t[1] for t in gbt.items])
        wgb_t, wgb_b = load_in(4)
        wgc_t, wgc_b = load_in(5)
        wci_t, wci_b = load_in(6)
        wgbv, wgcv, wciv = slot_view(wgb_t), slot_view(wgc_t), slot_view(wci_t)
        if ns:
            stg, b_stg = stage.next()
            c.dma("sp", lambda e, stg=stg: e.dma_start(out=stg[0:32, 0:512], in_=scc[l, :, :, :].rearrange("s k c -> (s k) c")),
                  dn(stg), writes=[b_stg])
            pt, pb = psum.next()
            for j in range(4):
                tr(pt[:, j * 32:(j + 1) * 32], stg[0:32, j * 128:(j + 1) * 128], ident[0:32, 0:32], [b_stg], pb, last=(j == 3))
            cp(stcfm[:, :, :], pt[:, 0:128].rearrange("p (j r) -> p j r", j=4), [pb], [b_stcfm], eng="act")
            ob = Buf("out")
            out_bufs.append(ob)
            c.dma("sp", lambda e: e.dma_start(out=o_ccs[l, :, 0, :], in_=scc[l, :, 1, :]), "d2d", writes=[ob])

        def ccw(j, k):
            o = COL["ccw"] + k * 4 + j
            return CA[:, l, o:o + 1]

        for j in range(4):
            xc_t, xc_b = xc.next()
            gb_t, gb_b = gbt.next()
            if first_group:
                c.op("dve", lambda e, xc_t=xc_t: e.memset(xc_t[:, 0:2], 0.0), writes=[xc_b])
            else:
                cp(xc_t[:, 0:2], carryC[:, l, j, :], [b_carryC], [xc_b], eng="dve")
            for (c0, n, kind) in tiles:
                p_gb, p_gbb = fm_block(wgbv, wgb_b, j, c0, n)
                pump()
                p_gc, p_gcb = fm_block(wgcv, wgc_b, j, c0, n)
                p_in, p_inb = fm_block(wciv, wci_b, j, c0, n)
                pump()
                cp(gb_t[:, c0:c0 + n], p_gb[:, 0:n], [p_gbb], [gb_b], eng="act")
                s_t, s_b = tf.next()
                cp(s_t[:, 0:n], p_in[:, 0:n], [p_inb], [s_b], eng="act")
                if kind == "p":
                    tt(xc_t[:, 2 + c0:2 + c0 + n], p_gc[:, 0:n], s_t[:, 0:n], ALU.mult, [p_gcb, s_b], [xc_b])
                else:
                    tt(xcn[:, j, :], p_gc[:, 0:n], s_t[:, 0:n], ALU.mult, [p_gcb, s_b], [b_xcn])
            if j == 1:
                ln_block()
            if not last_group:
                cp(carryC[:, l, j, :], xc_t[:, npr:npr + 2], [xc_b], [b_carryC], eng="dve")
            else:
                cp(xc32[:, j, :], xc_t[:, npr:npr + 2], [xc_b], [b_xc32], eng="dve")
            for (c0, n, kind) in ptiles:
                t_t, t_b = tf.next()
                ts(t_t[:, 0:n], xc_t[:, c0:c0 + n], ccw(j, 0), None, ALU.mult, None, [xc_b, b_CA], [t_b])
                stt(t_t[:, 0:n], xc_t[:, c0 + 1:c0 + 1 + n], ccw(j, 1), t_t[:, 0:n], ALU.mult, ALU.add, [xc_b, b_CA, t_b], [t_b])
                stt(t_t[:, 0:n], xc_t[:, c0 + 2:c0 + 2 + n], ccw(j, 2), t_t[:, 0:n], ALU.mult, ALU.add, [xc_b, b_CA, t_b], [t_b])
                tt(cop[:, j, c0:c0 + n], gb_t[:, c0:c0 + n], t_t[:, 0:n], ALU.mult, [gb_b, t_b], [b_cop])
            if ns:
                sm_t, sm_b = small.next()
                st3 = stcfm[:, j, :].rearrange("p (s k) -> p s k", k=2)
                ts(sm_t[:, 0:NS], st3[:, :, 0], ccw(j, 0), None, ALU.mult, None, [b_stcfm, b_CA], [sm_b])
                stt(sm_t[:, 0:NS], st3[:, :, 1], ccw(j, 1), sm_t[:, 0:NS], ALU.mult, ALU.add, [b_stcfm, b_CA, sm_b], [sm_b])
                stt(sm_t[:, 0:NS], xcn[:, j, :], ccw(j, 2), sm_t[:, 0:NS], ALU.mult, ALU.add, [b_xcn, b_CA, sm_b], [sm_b])
                tt(cop[:, j, npr:npr + NS], gb_t[:, npr:npr + NS], sm_t[:, 0:NS], ALU.mult, [gb_b, sm_b], [b_cop])
        if last_group:
            pt, pb = psum.next()
            for j in range(4):
                tr(pt[0:2, j * 128:(j + 1) * 128], xc32[:, j, :], ident[:, :], [b_xc32], pb, last=(j == 3))
            o_t, o_b = ostage.next()
            cp(o_t[0:2, :], pt[0:2, :], [pb], [o_b], eng="act")
            out_dma(o_ccp[l, :, :], o_t[0:2, :], [o_b], dn(o_t))
            pt, pb = psum.next()
            for j in range(4):
                tr(pt[0:NS, j * 128:(j + 1) * 128], xcn[:, j, :], ident[:, :], [b_xcn], pb, last=(j == 3))
            o_t, o_b = ostage.next()
            cp(o_t[0:NS, :], pt[0:NS, :], [pb], [o_b], eng="act")
            out_dma(o_ccs[l, :, 1, :], o_t[0:NS, :], [o_b], dn(o_t))

        if mstop == "C":
            return
        pump_att(len(asteps))
        if ns:
            while steps:
                pump()
            pacc, paccb = sacc["p"]
            r_t, r_b = small.next()
            c.op("dve", lambda e, r_t=r_t: e.reciprocal(out=r_t[:, 0:64], in_=pacc[:, 64:128]), reads=[paccb], writes=[r_b])
            tt(qm[:, :, npr:npr + NS], pacc[:, 0:64].rearrange("p (h s) -> p h s", h=4),
               r_t[:, 0:64].rearrange("p (s h) -> p h s", h=4), ALU.mult, [paccb, r_b], [b_qm])
            psum.release((pacc, paccb))

        if mstop == "M":
            return
        claim([b_merged])
        wg_all = W["w_branch_gate"][l].rearrange("(k p) (b c) -> p k b c", p=128, b=4)
        wouts = [W["gmlp_w_out"][l], W["conv_b_w_out"][l], W["conv_c_w_out"][l], W["mem_w_out"][l]]
        opsrc = [(uA, b_uA), (bop, b_bop), (cop, b_cop), (qm, b_qm)]
        for p2 in range(4):
            gsl = []
            for half_ in range(2):
                gsl.append(wload([(lambda t, i=i: t[:, i * 2048:(i + 1) * 2048].rearrange("p (k c) -> p k c", k=8),
                                   wg_all[:, :, half_ * 2 + i, p2 * 256:(p2 + 1) * 256]) for i in range(2)]))
            ow_t, ow_b = wload([(lambda t, br=br: t[:, br * 1024:(br + 1) * 1024].rearrange("p (k c) -> p k c", k=4),
                                 wouts[br].rearrange("(k p) c -> p k c", p=128)[:, :, p2 * 256:(p2 + 1) * 256]) for br in range(4)])
            for dmi in range(2):
                dm = p2 * 2 + dmi
                for (c0, n, _) in mtiles:
                    prods = []
                    for br in (0, 2, 3, 1):
                        gw_t, gw_b = gsl[br // 2]
                        gv = gw_t[:, (br % 2) * 2048:(br % 2 + 1) * 2048].rearrange("p (k c) -> p k c", k=8)
                        ov = ow_t[:, br * 1024:(br + 1) * 1024].rearrange("p (k c) -> p k c", k=4)
                        pg_, pgb_ = psum.next()
                        for k in range(KD):
                            mm(pg_[:, 0:n], gv[:, k, dmi * 128:(dmi + 1) * 128], h[:, k, c0:c0 + n], k == 0, k == KD - 1, [gw_b, b_h], pgb_, k == KD - 1)
                        po_, pob_ = psum.next()
                        o_t_, o_b_ = opsrc[br]
                        for k in range(4):
                            mm(po_[:, 0:n], ov[:, k, dmi * 128:(dmi + 1) * 128], o_t_[:, k, c0:c0 + n], k == 0, k == 3, [ow_b, o_b_], pob_, k == 3)
                        s_t, s_b = tf.next()
                        actf(s_t[:, 0:n], pg_[:, 0:n], AF.Sigmoid, [pgb_, b_CA], [s_b], bias=colA(l, "bg", br * 8 + dm), scale=1.0)
                        tt(s_t[:, 0:n], s_t[:, 0:n], po_[:, 0:n], ALU.mult, [s_b, pob_], [s_b])
                        prods.append((s_t, s_b))
                    (a0, a0b), (a1, a1b), (a2, a2b), (a3, a3b) = prods
                    tt(a0[:, 0:n], a0[:, 0:n], a1[:, 0:n], ALU.add, [a0b, a1b], [a0b])
                    tt(a2[:, 0:n], a2[:, 0:n], a3[:, 0:n], ALU.add, [a2b, a3b], [a2b])
                    tt(merged[:, dm, c0:c0 + n], a0[:, 0:n], a2[:, 0:n], ALU.add, [a0b, a2b], [b_merged])
        if mstop == "G":
            return
        wo = W["w_o"][l].rearrange("(k p) c -> p k c", p=128)
        wos = [wload([(slot_view, wo[:, :, half * 512:(half + 1) * 512])]) for half in range(2)]
        for ti, (c0, n, _) in enumerate(mtiles):
            for dm in range(KD):
                wo_t, wo_b = wos[dm // 4]
                wov = slot_view(wo_t)
                dd = dm % 4
                pt, pb = psum.next()
                for k in range(KD):
                    mm(pt[:, 0:n], wov[:, k, dd * 128:(dd + 1) * 128], merged[:, k, c0:c0 + n], k == 0, k == KD - 1, [wo_b, b_merged], pb, k == KD - 1)
                tt(x[:, dm, c0:c0 + n], x[:, dm, c0:c0 + n], pt[:, 0:n], ALU.add, [b_x2[dm][ti], pb], [b_x2[dm][ti]])

    for gi in (range(len(GROUPS)) if run_groups is None else run_groups):
        t0, npr, ns, tiles, mtiles = group_info(gi)
        for ci in range(npr // 128):
            stg, b_stg = stage.next()
            c.dma("sp", lambda e, stg=stg, r0=t0 + ci * 128: e.dma_start(out=stg[:, :], in_=xp[r0:r0 + 128, :]),
                  dn(stg), writes=[b_stg])
            for k4 in range(2):
                pt, pb = psum.next()
                for kk in range(4):
                    k = k4 * 4 + kk
                    tr(pt[:, kk * 128:(kk + 1) * 128], stg[:, k * 128:(k + 1) * 128], ident[:], [b_stg], pb, last=(kk == 3))
                ti_ = [i for i, (c0_, n_, _) in enumerate(mtiles) if c0_ <= ci * 128 < c0_ + n_][0]
                cp(x[:, k4 * 4:(k4 + 1) * 4, ci * 128:(ci + 1) * 128], pt[:, :].rearrange("p (k t) -> p k t", k=4), [pb],
                   [b_x2[k][ti_] for k in range(k4 * 4, (k4 + 1) * 4)], eng="act" if k4 == 0 else "dve")
        if ns:
            stg, b_stg = stage.next()
            c.dma("sp", lambda e, stg=stg: e.dma_start(out=stg[0:NS, :], in_=xs[:, :]), dn(stg), writes=[b_stg])
            pt, pb = psum.next()
            for k in range(KD):
                tr(pt[:, k * NS:(k + 1) * NS], stg[0:NS, k * 128:(k + 1) * 128], ident[0:NS, 0:NS], [b_stg], pb, last=(k == KD - 1))
            cp(x[:, :, npr:npr + NS], pt[:, 0:KD * NS].rearrange("p (k t) -> p k t", k=KD), [pb],
               [b_x2[k][len(mtiles) - 1] for k in range(KD)], eng="act")

        nl = L
        for l in range(nl):
            if stop == "load":
                break
            if l == 0:
                kv_a(0, gi)
            ffn(l, mtiles, W["ffn1_w_gate_up"], W["ffn1_w_down"], "ffn1_norm",
                hooks=(lambda l=l, gi=gi: kv_b(l, gi), None, lambda l=l, gi=gi: kv_b1(l, gi)))
            if stop == "ffn1":
                break
            if stop is not None and stop.startswith("mix:"):
                mixing(l, gi, t0, npr, ns, tiles, mtiles, mstop=stop[4:])
                break
            mixing(l, gi, t0, npr, ns, tiles, mtiles)
            if stop == "mix":
                break
            ffn(l, mtiles, W["ffn2_w_gate_up"], W["ffn2_w_down"], "ffn2_norm",
                hooks=(None, (lambda l=l, gi=gi: kv_a(l + 1, gi)) if l + 1 < L else None))
            if stop == "layer0":
                break

        claim([b_yb])
        for half in range(2):
            def outfn(k, c0, n, half=half):
                return yb[:, k - half * 4, c0:c0 + n]
            for ti, (c0, n, _) in enumerate(mtiles):
                for k in range(KD):
                    actf(sq[:, k, 0:n], x[:, k, c0:c0 + n], AF.Square, [b_x2[k][ti]], [b_sq])
                pt, pb = psum.next()
                for k in range(KD):
                    mm(pt[:, 0:n], ones_b[:, :], sq[:, k, 0:n], k == 0, k == KD - 1, [b_ones, b_sq], pb, k == KD - 1)
                r_t, r_b = tf.next()
                actf(r_t[:, 0:n], pt[:, 0:n], AF.Sqrt, [pb, b_epsc], [r_b], bias=epsc[:, 0:1], scale=1.0 / D)
                c.op("dve", lambda e, r_t=r_t, n=n: e.reciprocal(out=r_t[:, 0:n], in_=r_t[:, 0:n]), reads=[r_b], writes=[r_b])
                for k in range(half * 4, half * 4 + 4):
                    stt(outfn(k, c0, n), x[:, k, c0:c0 + n], FN[:, k:k + 1], r_t[:, 0:n], ALU.mult, ALU.mult,
                        [b_x2[k][ti], r_b, b_FN], [b_yb])
            cols = [(ci * 128, 128, t0 + ci * 128, "p") for ci in range(npr // 128)] + ([(npr, NS, 0, "s")] if ns else [])
            for (c0, n, r0, kind) in cols:
                pt, pb = psum.next()
                for kk in range(4):
                    tr(pt[0:n, kk * 128:(kk + 1) * 128], yb[:, kk, c0:c0 + n], ident[:, :], [b_yb], pb, last=(kk == 3))
                o_t, o_b = ostage.next()
                cp(o_t[0:n, :], pt[0:n, :], [pb], [o_b], eng="act")
                dst = (y_p[r0:r0 + n, half * 512:(half + 1) * 512] if kind == "p" else y_s[:, half * 512:(half + 1) * 512])
                out_dma(dst, o_t[0:n, :], [o_b], dn(o_t), q="act")

    c.wait_all("sp", out_bufs)
    c.finalize()
    return nc


_PROG = {}


def _get_prog():
    if "nc" not in _PROG:
        _PROG["nc"] = build_program()
    return _PROG["nc"]


WEIGHT_NAMES = [
    "ffn1_norm", "ffn1_w_gate_up", "ffn1_w_down", "mix_norm", "w_in", "gmlp_ln_g", "gmlp_ln_b", "gmlp_w_s",
    "gmlp_b_s", "gmlp_w_out", "conv_b_w", "conv_b_bias", "conv_b_ln_g", "conv_b_ln_b", "conv_b_w_out",
    "conv_c_w", "conv_c_w_out", "mem_norm", "mem_w_k", "mem_w_v", "mem_w_out", "w_branch_gate",
    "b_branch_gate", "w_o", "ffn2_norm", "ffn2_w_gate_up", "ffn2_w_down", "final_norm",
]


def make_in_maps(inputs, cores):
    f = lambda a: np.ascontiguousarray(np.asarray(a, dtype=np.float32))
    wts = {n: f(inputs[n]) for n in WEIGHT_NAMES}
    maps = []
    for b in cores:
        m = dict(wts)
        m["xp"] = f(inputs["x_prompt"][b])
        m["xs"] = f(inputs["x_sample"][b * NS:(b + 1) * NS, 0, :])
        m["mem"] = f(inputs["mem_prompt"][b])
        m["scb"] = f(inputs["state_conv_b"][:, b * NS:(b + 1) * NS])
        m["scc"] = f(inputs["state_conv_c"][:, b * NS:(b + 1) * NS])
        m["ck"] = f(np.asarray(inputs["cache_mem_k"])[:, b * NS:(b + 1) * NS].reshape(L, NS, NMEM, 512))
        m["cv"] = f(np.asarray(inputs["cache_mem_v"])[:, b * NS:(b + 1) * NS].reshape(L, NS, NMEM, 512))
        maps.append(m)
    return maps


def assemble(results):
    n = len(results)
    y_prompt = np.stack([r["y_p"] for r in results], 0)
    y_sample = np.concatenate([r["y_s"] for r in results], 0)[:, None, :]
    mk = np.stack([r["o_mk"] for r in results], 1).reshape(L, n, NMEM, 4, 128)
    mv = np.stack([r["o_mv"] for r in results], 1).reshape(L, n, NMEM, 4, 128)
    cbp = np.stack([r["o_cbp"] for r in results], 1)
    cbs = np.concatenate([r["o_cbs"] for r in results], 1)
    ccp = np.stack([r["o_ccp"] for r in results], 1)
    ccs = np.concatenate([r["o_ccs"] for r in results], 1)
    gv = np.concatenate([r["o_gv"] for r in results], 1)[:, :, None, :]
    outs = (y_prompt, y_sample, mk, mv, cbp, cbs, ccp, ccs, gv)
    return tuple(np.ascontiguousarray(o, dtype=np.float32) for o in outs)


def kernel(**inputs):
    nc = _get_prog()
    in_maps = make_in_maps(inputs, list(range(N_CORES)))
    res = run_bass_kernel_spmd(nc, in_maps, core_ids=list(range(N_CORES)))
    return assemble(res.results)
```

```python
import contextlib
import numpy as np
import concourse.bass as bass
import concourse.mybir as mybir
from concourse.bass_utils import run_bass_kernel_spmd

F32 = mybir.dt.float32
BF16 = mybir.dt.bfloat16
AF = mybir.ActivationFunctionType
ALU = mybir.AluOpType
AX = mybir.AxisListType

ENGS = ("pe", "act", "dve", "pool", "sp")
ENG_ATTR = {"pe": "tensor", "act": "scalar", "dve": "vector", "pool": "gpsimd", "sp": "sync"}
EPOCH_LIMIT = 24000

L = 4
D = 1024
KD = 8
DFF = 2816
NFF = 22
TP = 2048
NS = 16
NMEM = 256
EPS = 1e-6
GROUPS = [6, 5, 5]
XT = 768
NSLOT = 6
N_CORES = 8


class Buf:
    __slots__ = ("name", "w", "r")

    def __init__(self, name):
        self.name = name
        self.w = None
        self.r = {}


class _Rec:
    def __init__(self):
        self.calls = []

    def __getattr__(self, name):
        def f(*a, **k):
            self.calls.append((name, a, k))
            return self
        return f


def _record_call(fn):
    r = _Rec()
    fn(r)
    assert len(r.calls) == 1
    return r.calls[0]


class Ctx:
    def __init__(self, nc):
        self.nc = nc
        self.q = {e: [] for e in ENGS}
        self.cnt = {}
        self.epoch = {e: 0 for e in ENGS}
        self.waited = {e: {} for e in ENGS}
        self.semnames = []
        for e in ENGS:
            self._newsem((e, 0))

    def _newsem(self, key):
        self.cnt[key] = 0
        self.semnames.append(key)

    def dmasem(self, name):
        key = ("dma", name)
        if key not in self.cnt:
            self._newsem(key)
        return key

    def _emit_waits(self, eng, deps, rawdeps):
        cur = (eng, self.epoch[eng])
        for (s, v) in sorted(deps, key=str):
            if s == cur:
                if eng == "pe":
                    continue
                if (s, v) not in rawdeps:
                    continue
                if v < self.cnt[cur] - 1:
                    continue
            if self.waited[eng].get(s, 0) >= v:
                continue
            self.q[eng].append(("wait", s, v))
            self.waited[eng][s] = v

    @staticmethod
    def _deps(reads, writes):
        raw = set()
        deps = set()
        for b in reads:
            if b.w is not None:
                raw.add(b.w)
        for b in writes:
            if b.w is not None:
                deps.add(b.w)
            for s, v in b.r.items():
                deps.add((s, v))
        return deps | raw, raw

    @staticmethod
    def _record(tok, reads, writes):
        for b in reads:
            if b.r.get(tok[0], 0) < tok[1]:
                b.r[tok[0]] = tok[1]
        for b in writes:
            b.w = tok
            b.r = {}

    def op(self, eng, fn, reads=(), writes=(), inc=True):
        deps, raw = self._deps(reads, writes)
        self._emit_waits(eng, deps, raw)
        cur = (eng, self.epoch[eng])
        if inc:
            self.cnt[cur] += 1
            tok = (cur, self.cnt[cur])
        else:
            assert eng == "pe"
            tok = (cur, self.cnt[cur] + 1)
        self.q[eng].append(("op", _record_call(fn), cur if inc else None, self.cnt[cur] if inc else 0))
        self._record(tok, reads, writes)
        return tok

    def dma(self, qeng, fn, sem, reads=(), writes=()):
        deps, raw = self._deps(reads, writes)
        self._emit_waits(qeng, deps, raw)
        key = self.dmasem(sem)
        self.cnt[key] += 16
        tok = (key, self.cnt[key])
        self.q[qeng].append(("op", _record_call(fn), key, 16))
        self._record(tok, reads, writes)
        return tok

    def inherit(self, new_bufs, old_bufs):
        deps = {}
        for b in old_bufs:
            if b.w is not None and deps.get(b.w[0], 0) < b.w[1]:
                deps[b.w[0]] = b.w[1]
            for s, v in b.r.items():
                if deps.get(s, 0) < v:
                    deps[s] = v
        for nb in new_bufs:
            for s, v in deps.items():
                if nb.r.get(s, 0) < v:
                    nb.r[s] = v

    def wait_all(self, eng, bufs):
        deps = set()
        for b in bufs:
            if b.w is not None:
                deps.add(b.w)
            for s, v in b.r.items():
                deps.add((s, v))
        self._emit_waits(eng, deps, deps)

    def finalize(self):
        nc = self.nc
        needed = {}
        for eng in ENGS:
            for o in self.q[eng]:
                if o[0] == "wait" and o[1][0] != "dma":
                    needed.setdefault(o[1], set()).add(o[2])
        rank = {k: {v: i + 1 for i, v in enumerate(sorted(vs))} for k, vs in needed.items()}
        self.n_incs = {k: len(v) for k, v in needed.items()}
        with contextlib.ExitStack() as st:
            handles = {}
            for key in self.semnames:
                nm = "s_" + "_".join(str(k) for k in key)
                handles[key] = st.enter_context(nc.semaphore(nm))
            block = st.enter_context(nc.Block())
            for eng in ENGS:
                ops = self.q[eng]

                def run(e, ops=ops):
                    for o in ops:
                        if o[0] == "wait":
                            if o[1][0] == "dma":
                                e.wait_ge(handles[o[1]], o[2])
                            else:
                                e.wait_ge(handles[o[1]], rank[o[1]][o[2]])
                        else:
                            name, a, k = o[1]
                            ins = getattr(e, name)(*a, **k)
                            if o[2] is not None:
                                if o[2][0] == "dma":
                                    ins.then_inc(handles[o[2]], 16)
                                elif o[3] in rank.get(o[2], ()):
                                    ins.then_inc(handles[o[2]], 1)
                getattr(block, ENG_ATTR[eng])(run)


class Rot:
    def __init__(self, items):
        self.items = items
        self.i = 0
        self.held = set()

    def next(self):
        for _ in range(len(self.items) + 1):
            k = self.i
            self.i = (self.i + 1) % len(self.items)
            if k not in self.held:
                return self.items[k]
        raise RuntimeError("all held")

    def hold(self):
        t = self.next()
        self.held.add(self.items.index(t))
        return t

    def release(self, t):
        self.held.discard(self.items.index(t))


def group_info(gi):
    t0 = 128 * sum(GROUPS[:gi])
    npr = 128 * GROUPS[gi]
    ns = NS if gi == len(GROUPS) - 1 else 0
    tiles = []
    c = 0
    while c < npr:
        n = min(512, npr - c)
        tiles.append((c, n, "p"))
        c += n
    mtiles = list(tiles)
    if ns:
        tiles.append((npr, ns, "s"))
        lc0, ln_, _ = mtiles[-1]
        if ln_ + ns <= 512:
            mtiles[-1] = (lc0, ln_ + ns, "p")
        else:
            mtiles.append((npr, ns, "s"))
    return t0, npr, ns, tiles, mtiles


def build_program(stop=None, run_groups=None):
    nc = bass.Bass("TRN2", target_bir_lowering=False)
    c = Ctx(nc)

    def din(name, shape):
        return nc.dram_tensor(name, list(shape), F32, kind="ExternalInput").ap()

    def dout(name, shape):
        return nc.dram_tensor(name, list(shape), F32, kind="ExternalOutput").ap()

    xp = din("xp", [TP, D])
    xs = din("xs", [NS, D])
    mem = din("mem", [NMEM, D])
    scb = din("scb", [L, NS, 30, 512])
    scc = din("scc", [L, NS, 2, 512])
    ck = din("ck", [L, NS, NMEM, 512])
    cv = din("cv", [L, NS, NMEM, 512])
    W = {}
    for name, shape in [
        ("ffn1_norm", [L, D]), ("ffn1_w_gate_up", [L, D, 2 * DFF]), ("ffn1_w_down", [L, DFF, D]),
        ("mix_norm", [L, D]), ("w_in", [L, D, 4096]),
        ("gmlp_ln_g", [L, 512]), ("gmlp_ln_b", [L, 512]), ("gmlp_w_s", [L, 4, 128, 128]),
        ("gmlp_b_s", [L, 4, 128]), ("gmlp_w_out", [L, 512, D]),
        ("conv_b_w", [L, 31, 512]), ("conv_b_bias", [L, 512]), ("conv_b_ln_g", [L, 512]),
        ("conv_b_ln_b", [L, 512]), ("conv_b_w_out", [L, 512, D]),
        ("conv_c_w", [L, 3, 512]), ("conv_c_w_out", [L, 512, D]),
        ("mem_norm", [L, D]), ("mem_w_k", [L, D, 512]), ("mem_w_v", [L, D, 512]),
        ("mem_w_out", [L, 512, D]),
        ("w_branch_gate", [L, D, 4096]), ("b_branch_gate", [L, 4096]), ("w_o", [L, D, D]),
        ("ffn2_norm", [L, D]), ("ffn2_w_gate_up", [L, D, 2 * DFF]), ("ffn2_w_down", [L, DFF, D]),
        ("final_norm", [D]),
    ]:
        W[name] = din(name, shape)
    y_p = dout("y_p", [TP, D])
    y_s = dout("y_s", [NS, D])
    o_mk = dout("o_mk", [L, NMEM, 512])
    o_mv = dout("o_mv", [L, NMEM, 512])
    o_cbp = dout("o_cbp", [L, 30, 512])
    o_cbs = dout("o_cbs", [L, NS, 30, 512])
    o_ccp = dout("o_ccp", [L, 2, 512])
    o_ccs = dout("o_ccs", [L, NS, 2, 512])
    o_gv = dout("o_gv", [L, NS, 512])
    out_bufs = []

    tname = {}

    def sb(name, shape, dt):
        t = nc.alloc_sbuf_tensor(name, list(shape), dt)
        tname[id(t)] = name
        return t, Buf(name)

    def dn(t):
        return "d_" + tname[id(t)]

    x, _bx = sb("x", [128, KD, XT], F32)
    b_x2 = [[Buf(f"x{k}_{ti}") for ti in range(4)] for k in range(KD)]
    h, b_h = sb("h", [128, KD, XT], BF16)
    ring = Rot([sb(f"ring{i}", [128, 4096], BF16) for i in range(NSLOT)])
    ring_id = {id(t[0]): i for i, t in enumerate(ring.items)}
    tf = Rot([sb(f"tf{i}", [128, 512], F32) for i in range(6)])
    tb = Rot([sb(f"tb{i}", [128, 1024], BF16) for i in range(2)])
    sq, b_sq = sb("sq", [128, KD, 512], BF16)
    sq2, _ = sb("sq2", [128, KD, 272], BF16)
    b_sq2 = [Buf("sq2a"), Buf("sq2b")]

    def sqx(ti):
        if ti == 0:
            return sq, b_sq, 0
        if ti == 1:
            return sq2, b_sq2[0], 0
        return sq2, b_sq2[1], 256
    stage = Rot([sb(f"stage{i}", [128, D], F32) for i in range(2)])
    arenaA, _ = sb("arenaA", [128, 8 * XT], BF16)
    act = [(arenaA[:, i * 4 * XT:(i + 1) * 4 * XT].rearrange("p (j t) -> p j t", j=4), Buf(f"act{i}")) for i in range(2)]
    uA, b_uA = sb("uA", [128, 4, XT], BF16)
    regions = {"A": [], "V": [], "X": [], "B": []}

    def claimR(rn, new_bufs):
        c.inherit(new_bufs, regions[rn])
        regions[rn] = list(new_bufs)

    regV, _ = sb("regV", [128, 8 * 512], BF16)
    vtm, b_vtm = regV[:, :].rearrange("p (c d) -> p c d", c=8), Buf("vtm")
    diag_t, b_diag = regV[:, 0:31 * 128].rearrange("p (k c) -> p k c", k=31), Buf("diag")
    b_ybj = [Buf(f"yb{j}") for j in range(4)]
    memn, b_memn = regV[:, 0:KD * NMEM].rearrange("p (k m) -> p k m", k=KD), Buf("memn")
    XCW = 2 + XT
    xc = Rot([(regV[:, i * 1792:i * 1792 + 2 * XCW].bitcast(F32), Buf(f"xc{i}")) for i in range(2)])
    regX, _ = sb("regX", [128, 4 * (30 + XT)], BF16)
    xb, b_xb = regX[:, :].rearrange("p (j t) -> p j t", j=4), Buf("xb")
    gbt = Rot([(regX[:, i * 2 * XT:(i + 1) * 2 * XT].bitcast(F32), Buf(f"gbt{i}")) for i in range(2)])
    regB, _ = sb("regB", [128, 8 * XT], BF16)
    bop, b_bop = regB[:, 0:4 * XT].rearrange("p (j t) -> p j t", j=4), Buf("bop")
    cop, b_cop = regB[:, 4 * XT:8 * XT].rearrange("p (j t) -> p j t", j=4), Buf("cop")
    stfm, b_stfm = regB[:, 0:2 * 4 * NS * 30].bitcast(F32).rearrange("p (j r) -> p j r", j=4), Buf("stfm")
    yb, b_yb = arenaA[:, :].bitcast(F32).rearrange("p (j t) -> p j t", j=4), Buf("yb")
    qm, b_qm = sb("qm", [128, 4, XT], BF16)
    merged, b_merged = arenaA[:, :].rearrange("p (j t) -> p j t", j=KD), Buf("merged")

    def claim(new_bufs):
        claimR("A", new_bufs)
    ident, b_ident = sb("ident", [128, 128], F32)
    ones_b, b_ones = sb("ones_b", [128, 128], BF16)
    ones_row, b_onesrow = sb("ones_row", [1, 128], BF16)
    epsc, b_epsc = sb("epsc", [128, 1], F32)
    idb16, b_idb16 = sb("idb16", [NS, NS], BF16)
    CA, b_CA = sb("CA", [128, L, 128], F32)
    CB, b_CB = sb("CB", [128, L, 128], F32)
    FN, b_FN = sb("FN", [128, 8], F32)
    cstage = Rot([sb(f"cstage{i}", [128, 128], F32) for i in range(2)])
    carryB, b_carryB = sb("carryB", [128, L, 4, 30], BF16)
    carryC, b_carryC = sb("carryC", [128, L, 4, 2], F32)
    kT, b_kT = sb("kT", [128, 4, NMEM], BF16)
    vbf, b_vbf = sb("vbf", [128, 2, 512], BF16)
    wsT, b_wsT = sb("wsT", [128, 4, 128], BF16)
    brow, b_brow = sb("brow", [1, 512], BF16)
    glnbc, b_glnbc = sb("glnbc", [128, 2, 512], F32)
    ws00, b_ws00 = sb("ws00", [128, 8], F32)
    xb32, b_xb32 = sb("xb32", [128, 4, 32], F32)
    xc32, b_xc32 = sb("xc32", [128, 4, 2], F32)
    xbn, b_xbn = sb("xbn", [128, 4, NS], F32)
    xcn, b_xcn = sb("xcn", [128, 4, NS], F32)
    stcfm, b_stcfm = sb("stcfm", [128, 4, NS * 2], F32)
    qtm, b_qtm = sb("qtm", [NS, 512], BF16)
    kst = Rot([sb(f"kst{i}", [128, 2, 512], BF16) for i in range(2)])
    vst = Rot([sb(f"vst{i}", [128, 2, 512], BF16) for i in range(2)])
    prod, b_prod = sb("prod", [128, 2, 512], F32)
    small = Rot([sb(f"small{i}", [128, 64], F32) for i in range(4)])
    esm = Rot([sb(f"esm{i}", [128, 8], BF16) for i in range(2)])
    ostage = Rot([sb(f"ostage{i}", [128, 512], F32) for i in range(2)])
    print("sbuf bytes remaining:", nc.sbuf_bytes_remaining)

    psum = Rot([(nc.alloc_psum_tensor(f"ps{i}", [128, 512], F32), Buf(f"ps{i}")) for i in range(8)])

    def mm(out, lhsT, rhs, start, stop, reads, pbuf, last):
        c.op("pe", lambda e: e.matmul(out, lhsT, rhs, start=start, stop=stop),
             reads=reads, writes=[pbuf], inc=last)

    def tr(out, in_, idn, reads, pbuf, last=True):
        c.op("pe", lambda e: e.transpose(out, in_, idn), reads=list(reads) + [b_ident], writes=[pbuf], inc=last)

    def actf(out, in_, func, reads, writes, bias=None, scale=None):
        kw = {}
        if bias is not None:
            kw["bias"] = bias
        if scale is not None:
            kw["scale"] = scale
        c.op("act", lambda e: e.activation(out=out, in_=in_, func=func, **kw), reads=reads, writes=writes)

    def tt(out, in0, in1, op, reads, writes, eng="dve"):
        c.op(eng, lambda e: e.tensor_tensor(out=out, in0=in0, in1=in1, op=op), reads=reads, writes=writes)

    def ts(out, in0, s1, s2, op0, op1, reads, writes, eng="dve"):
        if s2 is None:
            c.op(eng, lambda e: e.tensor_scalar(out=out, in0=in0, scalar1=s1, scalar2=None, op0=op0),
                 reads=reads, writes=writes)
        else:
            c.op(eng, lambda e: e.tensor_scalar(out=out, in0=in0, scalar1=s1, scalar2=s2, op0=op0, op1=op1),
                 reads=reads, writes=writes)

    def stt(out, in0, scalar, in1, op0, op1, reads, writes, eng="dve"):
        c.op(eng, lambda e: e.scalar_tensor_tensor(out=out, in0=in0, scalar=scalar, in1=in1, op0=op0, op1=op1),
             reads=reads, writes=writes)

    def cp(out, in_, reads, writes, eng="dve"):
        if eng == "act":
            actf(out, in_, AF.Copy, reads, writes)
        else:
            c.op(eng, lambda e: e.tensor_copy(out=out, in_=in_), reads=reads, writes=writes)

    def wload(parts):
        st_, sb_ = ring.next()
        i = ring_id[id(st_)]
        for dst_fn, src in parts:
            dst = dst_fn(st_)
            c.dma("pool", lambda e, dst=dst, src=src: e.dma_start(out=dst, in_=src), f"ring{i}", writes=[sb_])
        return st_, sb_

    def out_dma(dst, src, reads, sem, q="act"):
        ob = Buf("out")
        out_bufs.append(ob)
        c.dma(q, lambda e: e.dma_start(out=dst, in_=src), sem, reads=reads, writes=[ob])

    c.op("pool", lambda e: e.memset(ident[:], 0.0), writes=[b_ident])
    c.op("pool", lambda e: e.affine_select(out=ident[:], in_=ident[:], compare_op=ALU.not_equal, fill=1.0,
                                           base=0, pattern=[[-1, 128]], channel_multiplier=1),
         reads=[b_ident], writes=[b_ident])
    c.op("dve", lambda e: e.memset(ones_b[:], 1.0), writes=[b_ones])
    c.op("dve", lambda e: e.memset(ones_row[:], 1.0), writes=[b_onesrow])
    c.op("dve", lambda e: e.memset(epsc[:], EPS), writes=[b_epsc])
    c.op("dve", lambda e: e.tensor_copy(out=idb16[:], in_=ident[0:NS, 0:NS]), reads=[b_ident], writes=[b_idb16])
    c.op("dve", lambda e: e.memset(carryC[:], 0.0), writes=[b_carryC])
    c.op("dve", lambda e: e.memset(carryB[:], 0.0), writes=[b_carryB])

    def load_cols(rows_list, dst_ap, nrows):
        stg, b_stg = cstage.next()
        c.op("dve", lambda e: e.memset(stg[:], 0.0), writes=[b_stg])
        for (r0, src) in rows_list:
            nr = src.shape[0]
            c.dma("sp", lambda e, r0=r0, nr=nr, src=src: e.dma_start(out=stg[r0:r0 + nr, :], in_=src),
                  dn(stg), writes=[b_stg])
        pt, pb = psum.next()
        tr(pt[:, 0:128], stg[:, :], ident[:], [b_stg], pb)
        cp(dst_ap, pt[:, 0:128], [pb], [b_CA, b_CB, b_FN], eng="act")

    def rows(v, n):
        return v.rearrange("(k p) -> k p", p=128)

    for l in range(L):
        load_cols([
            (0, rows(W["ffn1_norm"][l], 8)), (8, rows(W["mix_norm"][l], 8)),
            (16, rows(W["ffn2_norm"][l], 8)), (24, rows(W["mem_norm"][l], 8)),
            (32, rows(W["b_branch_gate"][l], 32)),
            (64, rows(W["conv_b_bias"][l], 4)), (68, rows(W["conv_b_ln_g"][l], 4)),
            (72, rows(W["conv_b_ln_b"][l], 4)),
            (76, W["conv_c_w"][l].rearrange("k (j p) -> (k j) p", p=128)),
        ], CA[:, l, :], 88)
        load_cols([(0, W["conv_b_w"][l].rearrange("k (j p) -> (k j) p", p=128))], CB[:, l, :], 124)
    stg, b_stg = cstage.next()
    c.op("dve", lambda e: e.memset(stg[:], 0.0), writes=[b_stg])
    c.dma("sp", lambda e: e.dma_start(out=stg[0:8, :], in_=rows(W["final_norm"], 8)), dn(stg), writes=[b_stg])
    pt, pb = psum.next()
    tr(pt[:, 0:128], stg[:, :], ident[:], [b_stg], pb)
    cp(FN[:, :], pt[:, 0:8], [pb], [b_FN], eng="act")

    COL = {"ffn1_norm": 0, "mix_norm": 8, "ffn2_norm": 16, "mem_norm": 24, "bg": 32,
           "cb_bias": 64, "cb_ln_g": 68, "cb_ln_b": 72, "ccw": 76}

    def colA(l, name, j):
        o = COL[name] + j
        return CA[:, l, o:o + 1]

    def rmsnorm(tiles, gcol_fn, out_fn, out_bufs_w):
        order = sorted(range(len(tiles)), key=lambda i: (tiles[i][1], i))
        for ti in order:
            c0, n, _ = tiles[ti]
            for k in range(KD):
                actf(sq[:, k, 0:n], x[:, k, c0:c0 + n], AF.Square, [b_x2[k][ti]], [b_sq])
            pt, pb = psum.next()
            for k in range(KD):
                mm(pt[:, 0:n], ones_b[:, :], sq[:, k, 0:n], k == 0, k == KD - 1, [b_ones, b_sq], pb, k == KD - 1)
            r_t, r_b = tf.next()
            actf(r_t[:, 0:n], pt[:, 0:n], AF.Sqrt, [pb, b_epsc], [r_b], bias=epsc[:, 0:1], scale=1.0 / D)
            c.op("dve", lambda e, r_t=r_t, n=n: e.reciprocal(out=r_t[:, 0:n], in_=r_t[:, 0:n]), reads=[r_b], writes=[r_b])
            for k in range(KD):
                stt(out_fn(k, c0, n), x[:, k, c0:c0 + n], gcol_fn(k), r_t[:, 0:n], ALU.mult, ALU.mult,
                    [b_x2[k][ti], r_b, b_CA, b_FN], out_bufs_w)

    def ffn(l, tiles, wgu, wdn, normname, hooks=None):
        claim([act[0][1], act[1][1]])
        if hooks and len(hooks) > 2 and hooks[2]:
            hooks[2]()
        rmsnorm(tiles, lambda k: colA(l, normname, k), lambda k, c0, n: h[:, k, c0:c0 + n], [b_h])
        if hooks and hooks[0]:
            hooks[0]()
        ffg = [(i, min(4, NFF - i)) for i in range(0, NFF, 4)]
        wv = wgu[l].rearrange("(k p) c -> p k c", p=128)

        def up(gi_, f0, G):
            sg_, bg_ = wload([(lambda t: t[:, 0:8 * G * 128].rearrange("p (k c) -> p k c", k=8),
                               wv[:, :, f0 * 128:(f0 + G) * 128])])
            su_, bu_ = wload([(lambda t: t[:, 0:8 * G * 128].rearrange("p (k c) -> p k c", k=8),
                               wv[:, :, DFF + f0 * 128:DFF + (f0 + G) * 128])])
            sgv = sg_[:, 0:8 * G * 128].rearrange("p (k c) -> p k c", k=8)
            suv = su_[:, 0:8 * G * 128].rearrange("p (k c) -> p k c", k=8)
            a_t, a_b = act[gi_ % 2]
            for j in range(G):
                for (c0, n, _) in sorted(tiles, key=lambda t_: t_[1]):
                    pg, pgb = psum.next()
                    pu, pub = psum.next()
                    for k in range(KD):
                        mm(pg[:, 0:n], sgv[:, k, j * 128:(j + 1) * 128], h[:, k, c0:c0 + n], k == 0, k == KD - 1,
                           [bg_, b_h], pgb, k == KD - 1)
                    for k in range(KD):
                        mm(pu[:, 0:n], suv[:, k, j * 128:(j + 1) * 128], h[:, k, c0:c0 + n], k == 0, k == KD - 1,
                           [bu_, b_h], pub, k == KD - 1)
                    s_t, s_b = tf.next()
                    actf(s_t[:, 0:n], pg[:, 0:n], AF.Silu, [pgb], [s_b])
                    tt(a_t[:, j, c0:c0 + n], s_t[:, 0:n], pu[:, 0:n], ALU.mult, [s_b, pub], [a_b])

        def down(gi_, f0, G):
            sd_, bd_ = wload([(lambda t: t[:, 0:G * 1024].rearrange("p (j c) -> p j c", j=G),
                               wdn[l][f0 * 128:(f0 + G) * 128, :].rearrange("(j p) c -> p j c", p=128))])
            sdv = sd_[:, 0:G * 1024].rearrange("p (j c) -> p j c", j=G)
            a_t, a_b = act[gi_ % 2]
            for ti, (c0, n, _) in enumerate(tiles):
                for dm in range(KD):
                    pt, pb = psum.next()
                    for j in range(G):
                        mm(pt[:, 0:n], sdv[:, j, dm * 128:(dm + 1) * 128], a_t[:, j, c0:c0 + n], j == 0, j == G - 1,
                           [bd_, a_b], pb, j == G - 1)
                    stt(x[:, dm, c0:c0 + n], pt[:, 0:n], 0.5, x[:, dm, c0:c0 + n], ALU.mult, ALU.add,
                        [pb, b_x2[dm][ti]], [b_x2[dm][ti]])

        up(0, *ffg[0])
        for i in range(len(ffg)):
            if i + 1 < len(ffg):
                up(i + 1, *ffg[i + 1])
            if hooks and hooks[1] and i == 0:
                hooks[1]()
            down(i, *ffg[i])

    kvst = {}

    def kv_a(l, gi):
        first_group = gi == 0
        kvst["stg"] = []
        for mc in range(2):
            stg, b_stg = stage.next()
            c.dma("sp", lambda e, stg=stg, mc=mc: e.dma_start(out=stg[:, :], in_=mem[mc * 128:(mc + 1) * 128, :]),
                  dn(stg), writes=[b_stg])
            sm_t, sm_b = small.next()
            c.op("dve", lambda e, sm_t=sm_t: e.memset(sm_t[:, 0:8], 0.0), writes=[sm_b])
            j_t, j_b = tf.next()
            for hh in range(2):
                c.op("act", lambda e, stg=stg, sm_t=sm_t, j_t=j_t, hh=hh: e.activation(
                    out=j_t[:, :], in_=stg[:, hh * 512:(hh + 1) * 512], func=AF.Square, accum_out=sm_t[:, hh:hh + 1]),
                    reads=[b_stg], writes=[j_b, sm_b])
            tt(sm_t[:, 2:3], sm_t[:, 0:1], sm_t[:, 1:2], ALU.add, [sm_b], [sm_b])
            actf(sm_t[:, 3:4], sm_t[:, 2:3], AF.Sqrt, [sm_b, b_epsc], [sm_b], bias=epsc[:, 0:1], scale=1.0 / D)
            c.op("dve", lambda e, sm_t=sm_t: e.reciprocal(out=sm_t[:, 4:5], in_=sm_t[:, 3:4]), reads=[sm_b], writes=[sm_b])
            ts(stg[:, :], stg[:, :], sm_t[:, 4:5], None, ALU.mult, None, [b_stg, sm_b], [b_stg])
            kvst["stg"].append((stg, b_stg))

    def kv_b1(l, gi):
        claimR("V", [b_memn])
        for mc in range(2):
            stg, b_stg = kvst["stg"][mc]
            for k4 in range(2):
                pt, pb = psum.next()
                for kk in range(4):
                    k = k4 * 4 + kk
                    tr(pt[:, kk * 128:(kk + 1) * 128], stg[:, k * 128:(k + 1) * 128], ident[:], [b_stg], pb, last=(kk == 3))
                for kk in range(4):
                    k = k4 * 4 + kk
                    ts(memn[:, k, mc * 128:(mc + 1) * 128], pt[:, kk * 128:(kk + 1) * 128], colA(l, "mem_norm", k), None,
                       ALU.mult, None, [pb, b_CA], [b_memn])

    def kv_b(l, gi):
        first_group = gi == 0
        slot_view = lambda t: t[:, 0:4096].rearrange("p (k c) -> p k c", k=8)
        wk_t, wk_b = wload([(slot_view, W["mem_w_k"][l].rearrange("(k p) c -> p k c", p=128))])
        wv_t, wv_b = wload([(slot_view, W["mem_w_v"][l].rearrange("(k p) c -> p k c", p=128))])
        wkv = slot_view(wk_t)
        wvv = slot_view(wv_t)
        for mc in range(2):
            for (wt, wb_, o_dram, is_v) in ((wkv, wk_b, o_mk, False), (wvv, wv_b, o_mv, True)):
                pt, pb = psum.next()
                for k in range(KD):
                    mm(pt[:, :], memn[:, k, mc * 128:(mc + 1) * 128], wt[:, k, :], k == 0, k == KD - 1, [b_memn, wb_], pb, k == KD - 1)
                if first_group:
                    o_t, o_b = ostage.next()
                    cp(o_t[:, :], pt[:, :], [pb], [o_b, pb], eng="act")
                    out_dma(o_dram[l, mc * 128:(mc + 1) * 128, :], o_t[:, :], [o_b], dn(o_t), q="act")
                if is_v:
                    cp(vbf[:, mc, :], pt[:, :], [pb], [b_vbf, pb], eng="dve")
        for hd in range(4):
            pt, pb = psum.next()
            for k in range(KD):
                mm(pt[:, 0:NMEM], wkv[:, k, hd * 128:(hd + 1) * 128], memn[:, k, :], k == 0, k == KD - 1, [wk_b, b_memn], pb, k == KD - 1)
            cp(kT[:, hd, :], pt[:, 0:NMEM], [pb], [b_kT], eng="act")


    def mixing(l, gi, t0, npr, ns, tiles, mtiles, mstop=None):
        last_group = gi == len(GROUPS) - 1
        first_group = gi == 0
        ptiles = [t for t in tiles if t[2] == "p"]
        stile = [t for t in tiles if t[2] == "s"]
        nch = npr // 128
        rmsnorm(mtiles, lambda k: colA(l, "mix_norm", k), lambda k, c0, n: h[:, k, c0:c0 + n], [b_h])

        if mstop == "kv":
            return
        wsr_t, b_wsraw = stage.next()
        wsraw = wsr_t[:, 0:512].rearrange("p (g j) -> p g j", g=4)
        c.dma("sp", lambda e: e.dma_start(out=wsraw, in_=W["gmlp_w_s"][l].rearrange("g i j -> i g j")),
              dn(wsr_t), writes=[b_wsraw])
        for g in range(4):
            c.op("pool", lambda e, g=g: e.affine_select(out=wsraw[:, g, :], in_=wsraw[:, g, :], compare_op=ALU.is_ge,
                                                        fill=0.0, base=0, pattern=[[-1, 128]], channel_multiplier=1),
                 reads=[b_wsraw], writes=[b_wsraw])
        pt, pb = psum.next()
        for g in range(4):
            tr(pt[:, g * 128:(g + 1) * 128], wsraw[:, g, :], ident[:], [b_wsraw], pb, last=(g == 3))
        cp(wsT[:, :, :], pt[:, :].rearrange("p (g i) -> p g i", g=4), [pb], [b_wsT], eng="act")
        c.dma("pool", lambda e: e.dma_start(out=brow[:, :], in_=W["gmlp_b_s"][l].rearrange("g i -> (g i)").rearrange("(o n) -> o n", o=1)),
              "brow", writes=[b_brow])
        c.dma("act", lambda e: e.dma_start(out=glnbc[:, 0, :], in_=W["gmlp_ln_g"][l:l + 1, :].partition_broadcast(128)),
              "glnbc", writes=[b_glnbc])
        c.dma("act", lambda e: e.dma_start(out=glnbc[:, 1, :], in_=W["gmlp_ln_b"][l:l + 1, :].partition_broadcast(128)),
              "glnbc", writes=[b_glnbc])
        if ns:
            for g in range(4):
                c.dma("act", lambda e, g=g: e.dma_start(out=ws00[:, g:g + 1], in_=W["gmlp_w_s"][l, g, 0:1, 0:1].partition_broadcast(128)),
                      "ws00", writes=[b_ws00])
                c.dma("act", lambda e, g=g: e.dma_start(out=ws00[:, 4 + g:5 + g], in_=W["gmlp_b_s"][l, g:g + 1, 0:1].partition_broadcast(128)),
                      "ws00", writes=[b_ws00])

        slot_view = lambda t: t[:, 0:4096].rearrange("p (k c) -> p k c", k=8)
        win = W["w_in"][l].rearrange("(k p) c -> p k c", p=128)

        def load_in(blk):
            return wload([(slot_view, win[:, :, blk * 512:(blk + 1) * 512])])

        def fm_block(wt_v, wb_, j, c0, n):
            pt, pb = psum.next()
            for k in range(KD):
                mm(pt[:, 0:n], wt_v[:, k, j * 128:(j + 1) * 128], h[:, k, c0:c0 + n], k == 0, k == KD - 1, [wb_, b_h], pb, k == KD - 1)
            return pt, pb

        w0_t, w0_b = load_in(0)
        w0v = slot_view(w0_t)
        for (c0, n, _) in sorted(mtiles, key=lambda t_: t_[1]):
            for j in range(4):
                pt, pb = fm_block(w0v, w0_b, j, c0, n)
                actf(uA[:, j, c0:c0 + n], pt[:, 0:n], AF.Gelu, [pb], [b_uA])
        w1_t, w1_b = load_in(1)
        w1v = slot_view(w1_t)
        claimR("V", [b_vtm])
        vchunks = [(ci * 128, 128, ci) for ci in range(nch)] + ([(npr, ns, nch)] if ns else [])
        for (c0, n, ci) in vchunks:
            pt, pb = psum.next()
            for k in range(KD):
                mm(pt[0:n, :], h[:, k, c0:c0 + n], w1v[:, k, :], k == 0, k == KD - 1, [b_h, w1_b], pb, k == KD - 1)
            g_t, g_b = tf.next()
            actf(g_t[0:n, :], pt[0:n, :], AF.Gelu, [pb], [g_b])
            sm_t, sm_b = small.next()
            c.op("dve", lambda e, g_t=g_t, sm_t=sm_t, n=n: e.bn_stats(out=sm_t[0:n, 0:6], in_=g_t[0:n, :]), reads=[g_b], writes=[sm_b])
            c.op("dve", lambda e, sm_t=sm_t, n=n: e.bn_aggr(out=sm_t[0:n, 8:10], in_=sm_t[0:n, 0:6]), reads=[sm_b], writes=[sm_b])
            actf(sm_t[0:n, 10:11], sm_t[0:n, 9:10], AF.Sqrt, [sm_b, b_epsc], [sm_b], bias=epsc[0:n, 0:1], scale=1.0)
            c.op("dve", lambda e, sm_t=sm_t, n=n: e.reciprocal(out=sm_t[0:n, 11:12], in_=sm_t[0:n, 10:11]), reads=[sm_b], writes=[sm_b])
            stt(sm_t[0:n, 12:13], sm_t[0:n, 8:9], -1.0, sm_t[0:n, 11:12], ALU.mult, ALU.mult, [sm_b], [sm_b])
            ts(g_t[0:n, :], g_t[0:n, :], sm_t[0:n, 11:12], sm_t[0:n, 12:13], ALU.mult, ALU.add, [g_b, sm_b], [g_b])
            tt(g_t[0:n, :], g_t[0:n, :], glnbc[0:n, 0, :], ALU.mult, [g_b, b_glnbc], [g_b])
            if ci < nch:
                tt(vtm[0:n, ci, :], g_t[0:n, :], glnbc[0:n, 1, :], ALU.add, [g_b, b_glnbc], [b_vtm])
            else:
                tt(g_t[0:n, :], g_t[0:n, :], glnbc[0:n, 1, :], ALU.add, [g_b, b_glnbc], [g_b])
                out_dma(o_gv[l, :, :], g_t[0:n, :], [g_b], dn(g_t))
                cp(vtm[0:NS, nch, :], g_t[0:NS, :], [g_b], [b_vtm], eng="dve")
        if mstop == "A":
            return
        wq_t, wq_b = load_in(7)
        wqv = slot_view(wq_t)
        qscale = 128 ** -0.5
        for hd in range(4):
            for (c0, n, _) in mtiles:
                pt, pb = fm_block(wqv, wq_b, hd, c0, n)
                actf(qm[:, hd, c0:c0 + n], pt[:, 0:n], AF.Identity, [pb], [b_qm], scale=qscale)
        if ns:
            pt, pb = psum.next()
            for k in range(KD):
                mm(pt[0:NS, :], h[:, k, npr:npr + NS], wqv[:, k, :], k == 0, k == KD - 1, [b_h, wq_b], pb, k == KD - 1)
            actf(qtm[:, :], pt[0:NS, :], AF.Identity, [pb], [b_qtm], scale=qscale)
        steps = []
        sacc = {}

        def pump():
            if steps:
                steps.pop(0)()

        if ns:
            sacc["p"] = psum.hold()
            pend = {}

            def step_a(s_):
                pacc, paccb = sacc["p"]
                k_t, k_b = kst.next()
                v_t, v_b = vst.next()
                c.dma("pool", lambda e: e.dma_start(out=k_t[:, :, :], in_=ck[l, s_].rearrange("(mc p) c -> p mc c", p=128)),
                      dn(k_t), writes=[k_b])
                c.dma("pool", lambda e: e.dma_start(out=v_t[:, :, :], in_=cv[l, s_].rearrange("(mc p) c -> p mc c", p=128)),
                      dn(v_t), writes=[v_b])
                pq, pqb = psum.next()
                mm(pq[:, :], idb16[:, s_:s_ + 1].broadcast_to([NS, 128]), qtm[:, :], True, True, [b_idb16, b_qtm], pqb, True)
                tt(prod[:, :, :], k_t[:, :, :], pq[:, None, :].broadcast_to([128, 2, 512]), ALU.mult, [k_b, pqb], [b_prod])
                sm_t, sm_b = small.next()
                c.op("dve", lambda e: e.tensor_reduce(out=sm_t[:, 0:8], in_=prod[:, :, :].rearrange("p m (h d) -> p (m h) d", h=4),
                                                      axis=AX.X, op=ALU.add), reads=[b_prod], writes=[sm_b])
                e_t, e_b = esm.next()
                actf(e_t[:, 0:8], sm_t[:, 0:8], AF.Exp, [sm_b], [e_b])
                pend[s_] = (v_t, v_b, e_t, e_b)

            def step_b(s_):
                pacc, paccb = sacc["p"]
                v_t, v_b, e_t, e_b = pend.pop(s_)
                for hd in range(4):
                    for mc in range(2):
                        mm(pacc[:, hd * NS + s_:hd * NS + s_ + 1], v_t[:, mc, hd * 128:(hd + 1) * 128], e_t[:, mc * 4 + hd:mc * 4 + hd + 1],
                           mc == 0, mc == 1, [v_b, e_b], paccb, False)
                for mc in range(2):
                    mm(pacc[:, 64 + s_ * 4:64 + s_ * 4 + 4], ones_b[:, :], e_t[:, mc * 4:mc * 4 + 4], mc == 0, mc == 1, [b_ones, e_b], paccb, mc == 1)

            steps.append(lambda: step_a(0))
            for s_ in range(NS):
                if s_ + 1 < NS:
                    steps.append(lambda s_=s_: step_a(s_ + 1))
                steps.append(lambda s_=s_: step_b(s_))

        claim(b_ybj)
        claimR("X", [b_xb])
        if ns:
            claimR("B", [b_stfm])
        wv_t2, wv_b2 = load_in(2)
        wg_t2, wg_b2 = load_in(3)
        wvalv = slot_view(wv_t2)
        wgatv = slot_view(wg_t2)
        if first_group:
            c.op("dve", lambda e: e.memset(xb[:, :, 0:30], 0.0), writes=[b_xb])
        else:
            cp(xb[:, :, 0:30], carryB[:, l, :, :], [b_carryB], [b_xb], eng="dve")
        if ns:
            for i4 in range(4):
                stg, b_stg = stage.next()
                c.dma("sp", lambda e, stg=stg, i4=i4: e.dma_start(
                    out=stg[0:120, 0:512], in_=scb[l, i4 * 4:(i4 + 1) * 4, :, :].rearrange("s k c -> (s k) c")),
                    dn(stg), writes=[b_stg])
                pt, pb = psum.next()
                for j in range(4):
                    tr(pt[:, j * 120:(j + 1) * 120], stg[0:120, j * 128:(j + 1) * 128], ident[0:120, 0:120], [b_stg], pb, last=(j == 3))
                cp(stfm[:, :, i4 * 120:(i4 + 1) * 120], pt[:, 0:480].rearrange("p (j r) -> p j r", j=4), [pb], [b_stfm], eng="act")
            ob = Buf("out")
            out_bufs.append(ob)
            c.dma("sp", lambda e: e.dma_start(out=o_cbs[l, :, 0:29, :], in_=scb[l, :, 1:30, :]), "d2d", writes=[ob])
        for j in range(4):
            for (c0, n, kind) in tiles:
                pv_, pvb_ = fm_block(wvalv, wv_b2, j, c0, n)
                pg_, pgb_ = fm_block(wgatv, wg_b2, j, c0, n)
                pump()
                s_t, s_b = tf.next()
                actf(s_t[:, 0:n], pg_[:, 0:n], AF.Sigmoid, [pgb_], [s_b])
                if kind == "p":
                    tt(xb[:, j, 30 + c0:30 + c0 + n], pv_[:, 0:n], s_t[:, 0:n], ALU.mult, [pvb_, s_b], [b_xb])
                    if last_group and c0 + n == npr:
                        tt(xb32[:, j, 0:30], pv_[:, n - 30:n], s_t[:, n - 30:n], ALU.mult, [pvb_, s_b], [b_xb32])
                else:
                    tt(xbn[:, j, :], pv_[:, 0:n], s_t[:, 0:n], ALU.mult, [pvb_, s_b], [b_xbn])
        if not last_group:
            cp(carryB[:, l, :, :], xb[:, :, npr:npr + 30], [b_xb], [b_carryB], eng="dve")
        if ns:
            pv, pvb = psum.next()
            for g in range(4):
                mm(pv[:, g * NS:(g + 1) * NS], vtm[0:NS, nch, g * 128:(g + 1) * 128], idb16[:, :], True, True, [b_vtm, b_idb16], pvb, g == 3)
            m_t, m_b = small.next()
            for g in range(4):
                ts(m_t[:, g * NS:(g + 1) * NS], pv[:, g * NS:(g + 1) * NS], ws00[:, g:g + 1], ws00[:, 4 + g:5 + g],
                   ALU.mult, ALU.add, [pvb, b_ws00], [m_b])
            tt(uA[:, :, npr:npr + NS], uA[:, :, npr:npr + NS], m_t[:, 0:4 * NS].rearrange("p (g s) -> p g s", g=4),
               ALU.mult, [b_uA, m_b], [b_uA])
        for ci in range(nch):
            pt, pb = psum.next()
            for g in range(4):
                mm(pt[:, g * 128:(g + 1) * 128], vtm[:, ci, g * 128:(g + 1) * 128], wsT[:, g, :], True, False, [b_vtm, b_wsT], pb, False)
                mm(pt[:, g * 128:(g + 1) * 128], ones_row[0:1, :], brow[0:1, g * 128:(g + 1) * 128], False, True, [b_onesrow, b_brow], pb, g == 3)
            tt(uA[:, :, ci * 128:(ci + 1) * 128], uA[:, :, ci * 128:(ci + 1) * 128],
               pt[:, :].rearrange("p (g i) -> p g i", g=4), ALU.mult, [b_uA, pb], [b_uA])

        def ln_prep(j, ti, c0, n, off=0):
            q_t, q_b, o = sqx(ti)
            o += off
            cp(q_t[:, j, o:o + n], yb[:, j, c0:c0 + n], [b_ybj[j]], [q_b], eng="dve")
            actf(q_t[:, 4 + j, o:o + n], yb[:, j, c0:c0 + n], AF.Square, [b_ybj[j]], [q_b])

        asteps = []

        def pump_att(k_):
            for _ in range(k_):
                if asteps:
                    asteps.pop(0)()

        att_items = [(hd, c0, n) for hd in range(4) for (c0, n, kind) in ptiles]
        att_state = {}

        def att_qk(i):
            hd, c0, n = att_items[i]
            e_t, e_b = tb.next()
            for mc in range(2):
                pt, pb = psum.next()
                mm(pt[:, 0:n], kT[:, hd, mc * 128:(mc + 1) * 128], qm[:, hd, c0:c0 + n], True, True, [b_kT, b_qm], pb, True)
                actf(e_t[:, mc * 512:mc * 512 + n], pt[:, 0:n], AF.Exp, [pb], [e_b])
            att_state[i] = (e_t, e_b)

        def att_pv(i):
            hd, c0, n = att_items[i]
            e_t, e_b = att_state.pop(i)
            po, pob = psum.next()
            pd, pdb = psum.next()
            for mc in range(2):
                mm(po[:, 0:n], vbf[:, mc, hd * 128:(hd + 1) * 128], e_t[:, mc * 512:mc * 512 + n], mc == 0, mc == 1, [b_vbf, e_b], pob, mc == 1)
            for mc in range(2):
                mm(pd[:, 0:n], ones_b[:, :], e_t[:, mc * 512:mc * 512 + n], mc == 0, mc == 1, [b_ones, e_b], pdb, mc == 1)
            r_t, r_b = tf.next()
            c.op("dve", lambda e: e.reciprocal(out=r_t[:, 0:n], in_=pd[:, 0:n]), reads=[pdb], writes=[r_b])
            tt(qm[:, hd, c0:c0 + n], po[:, 0:n], r_t[:, 0:n], ALU.mult, [pob, r_b], [b_qm])

        asteps.append(lambda: att_qk(0))
        for i in range(len(att_items)):
            if i + 1 < len(att_items):
                asteps.append(lambda i=i: att_qk(i + 1))
            asteps.append(lambda i=i: att_pv(i))

        claimR("V", [b_diag])
        wv3 = CB[:, l, :].rearrange("p (k j) -> p j k", j=4)
        for j in range(4):
            tt(diag_t[:, :, :], ident[:, :].unsqueeze(1).broadcast_to([128, 31, 128]),
               wv3[:, j, 0:31].unsqueeze(2).broadcast_to([128, 31, 128]), ALU.mult, [b_ident, b_CB], [b_diag])
            for ti, (c0, n, kind) in enumerate(ptiles):
                pump_att(1)
                pt, pb = psum.next()
                for k in range(31):
                    mm(pt[:, 0:n], diag_t[:, k, :], xb[:, j, c0 + k:c0 + k + n], k == 0, k == 30, [b_diag, b_xb], pb, k == 30)
                actf(yb[:, j, c0:c0 + n], pt[:, 0:n], AF.Identity, [pb, b_CA], [b_ybj[j]], bias=colA(l, "cb_bias", j), scale=1.0)
                ln_prep(j, ti, c0, n)
                pump_att(1)
            if ns:
                tt(prod[:, :, :].rearrange("p a b -> p (a b)")[:, 0:NS * 30].rearrange("p (s k) -> p s k", k=30), stfm[:, j, :].rearrange("p (s k) -> p s k", k=30),
                   wv3[:, j:j + 1, 0:30].broadcast_to([128, NS, 30]), ALU.mult, [b_stfm, b_CB], [b_prod])
                sm_t, sm_b = small.next()
                c.op("dve", lambda e, sm_t=sm_t: e.tensor_reduce(out=sm_t[:, 0:NS], in_=prod[:, :, :].rearrange("p a b -> p (a b)")[:, 0:NS * 30].rearrange("p (s k) -> p s k", k=30),
                                                                 axis=AX.X, op=ALU.add), reads=[b_prod], writes=[sm_b])
                stt(sm_t[:, 0:NS], xbn[:, j, :], CB[:, l, 30 * 4 + j:30 * 4 + j + 1], sm_t[:, 0:NS], ALU.mult, ALU.add,
                    [b_xbn, b_CB, sm_b], [sm_b])
                ts(yb[:, j, npr:npr + NS], sm_t[:, 0:NS], colA(l, "cb_bias", j), None, ALU.add, None, [sm_b, b_CA], [b_ybj[j]])
                ln_prep(j, len(ptiles) - 1, npr, NS, off=npr - ptiles[-1][0])
        if last_group:
            pt, pb = psum.next()
            for j in range(4):
                tr(pt[0:30, j * 128:(j + 1) * 128], xb32[:, j, 0:30], ident[:, :], [b_xb32], pb, last=(j == 3))
            o_t, o_b = ostage.next()
            cp(o_t[0:30, :], pt[0:30, :], [pb], [o_b], eng="act")
            out_dma(o_cbp[l, :, :], o_t[0:30, :], [o_b], dn(o_t))
            pt, pb = psum.next()
            for j in range(4):
                tr(pt[0:NS, j * 128:(j + 1) * 128], xbn[:, j, :], ident[:, :], [b_xbn], pb, last=(j == 3))
            o_t, o_b = ostage.next()
            cp(o_t[0:NS, :], pt[0:NS, :], [pb], [o_b], eng="act")
            out_dma(o_cbs[l, :, 29, :], o_t[0:NS, :], [o_b], dn(o_t))
        if mstop == "B":
            return
        def ln_block():
            for ti, (c0, n, _) in enumerate(mtiles):
                q_t, q_b, o = sqx(ti)
                p1, p1b = psum.next()
                p2, p2b = psum.next()
                for j in range(4):
                    mm(p1[:, 0:n], ones_b[:, :], q_t[:, j, o:o + n], j == 0, j == 3, [b_ones, q_b], p1b, j == 3)
                for j in range(4):
                    mm(p2[:, 0:n], ones_b[:, :], q_t[:, 4 + j, o:o + n], j == 0, j == 3, [b_ones, q_b], p2b, j == 3)
                mean_t, mean_b = tf.next()
                var_t, var_b = tf.next()
                cp(mean_t[:, 0:n], p1[:, 0:n], [p1b], [mean_b], eng="act")
                stt(var_t[:, 0:n], mean_t[:, 0:n], 1.0 / 512, mean_t[:, 0:n], ALU.mult, ALU.mult, [mean_b], [var_b])
                tt(var_t[:, 0:n], p2[:, 0:n], var_t[:, 0:n], ALU.subtract, [p2b, var_b], [var_b])
                actf(var_t[:, 0:n], var_t[:, 0:n], AF.Sqrt, [var_b, b_epsc], [var_b], bias=epsc[:, 0:1], scale=1.0 / 512)
                c.op("dve", lambda e, var_t=var_t, n=n: e.reciprocal(out=var_t[:, 0:n], in_=var_t[:, 0:n]), reads=[var_b], writes=[var_b])
                for j in range(4):
                    t_t, t_b = tf.next()
                    stt(t_t[:, 0:n], mean_t[:, 0:n], -1.0 / 512, yb[:, j, c0:c0 + n], ALU.mult, ALU.add, [mean_b, b_ybj[j]], [t_b])
                    tt(t_t[:, 0:n], t_t[:, 0:n], var_t[:, 0:n], ALU.mult, [t_b, var_b], [t_b])
                    actf(bop[:, j, c0:c0 + n], t_t[:, 0:n], AF.Silu, [t_b, b_CA], [b_bop],
                         bias=colA(l, "cb_ln_b", j), scale=colA(l, "cb_ln_g", j))


        claimR("B", [b_bop, b_cop])
        claimR("V", [t[1] for t in xc.items])
        claimR("X", [t[1] for t in gbt.items])
        wgb_t, wgb_b = load_in(4)
        wgc_t, wgc_b = load_in(5)
        wci_t, wci_b = load_in(6)
        wgbv, wgcv, wciv = slot_view(wgb_t), slot_view(wgc_t), slot_view(wci_t)
        if ns:
            stg, b_stg = stage.next()
            c.dma("sp", lambda e, stg=stg: e.dma_start(out=stg[0:32, 0:512], in_=scc[l, :, :, :].rearrange("s k c -> (s k) c")),
                  dn(stg), writes=[b_stg])
            pt, pb = psum.next()
            for j in range(4):
                tr(pt[:, j * 32:(j + 1) * 32], stg[0:32, j * 128:(j + 1) * 128], ident[0:32, 0:32], [b_stg], pb, last=(j == 3))
            cp(stcfm[:, :, :], pt[:, 0:128].rearrange("p (j r) -> p j r", j=4), [pb], [b_stcfm], eng="act")
            ob = Buf("out")
            out_bufs.append(ob)
            c.dma("sp", lambda e: e.dma_start(out=o_ccs[l, :, 0, :], in_=scc[l, :, 1, :]), "d2d", writes=[ob])

        def ccw(j, k):
            o = COL["ccw"] + k * 4 + j
            return CA[:, l, o:o + 1]

        for j in range(4):
            xc_t, xc_b = xc.next()
            gb_t, gb_b = gbt.next()
            if first_group:
                c.op("dve", lambda e, xc_t=xc_t: e.memset(xc_t[:, 0:2], 0.0), writes=[xc_b])
            else:
                cp(xc_t[:, 0:2], carryC[:, l, j, :], [b_carryC], [xc_b], eng="dve")
            for (c0, n, kind) in tiles:
                p_gb, p_gbb = fm_block(wgbv, wgb_b, j, c0, n)
                pump()
                p_gc, p_gcb = fm_block(wgcv, wgc_b, j, c0, n)
                p_in, p_inb = fm_block(wciv, wci_b, j, c0, n)
                pump()
                cp(gb_t[:, c0:c0 + n], p_gb[:, 0:n], [p_gbb], [gb_b], eng="act")
                s_t, s_b = tf.next()
                cp(s_t[:, 0:n], p_in[:, 0:n], [p_inb], [s_b], eng="act")
                if kind == "p":
                    tt(xc_t[:, 2 + c0:2 + c0 + n], p_gc[:, 0:n], s_t[:, 0:n], ALU.mult, [p_gcb, s_b], [xc_b])
                else:
                    tt(xcn[:, j, :], p_gc[:, 0:n], s_t[:, 0:n], ALU.mult, [p_gcb, s_b], [b_xcn])
            if j == 1:
                ln_block()
            if not last_group:
                cp(carryC[:, l, j, :], xc_t[:, npr:npr + 2], [xc_b], [b_carryC], eng="dve")
            else:
                cp(xc32[:, j, :], xc_t[:, npr:npr + 2], [xc_b], [b_xc32], eng="dve")
            for (c0, n, kind) in ptiles:
                t_t, t_b = tf.next()
                ts(t_t[:, 0:n], xc_t[:, c0:c0 + n], ccw(j, 0), None, ALU.mult, None, [xc_b, b_CA], [t_b])
                stt(t_t[:, 0:n], xc_t[:, c0 + 1:c0 + 1 + n], ccw(j, 1), t_t[:, 0:n], ALU.mult, ALU.add, [xc_b, b_CA, t_b], [t_b])
                stt(t_t[:, 0:n], xc_t[:, c0 + 2:c0 + 2 + n], ccw(j, 2), t_t[:, 0:n], ALU.mult, ALU.add, [xc_b, b_CA, t_b], [t_b])
                tt(cop[:, j, c0:c0 + n], gb_t[:, c0:c0 + n], t_t[:, 0:n], ALU.mult, [gb_b, t_b], [b_cop])
            if ns:
                sm_t, sm_b = small.next()
                st3 = stcfm[:, j, :].rearrange("p (s k) -> p s k", k=2)
                ts(sm_t[:, 0:NS], st3[:, :, 0], ccw(j, 0), None, ALU.mult, None, [b_stcfm, b_CA], [sm_b])
                stt(sm_t[:, 0:NS], st3[:, :, 1], ccw(j, 1), sm_t[:, 0:NS], ALU.mult, ALU.add, [b_stcfm, b_CA, sm_b], [sm_b])
                stt(sm_t[:, 0:NS], xcn[:, j, :], ccw(j, 2), sm_t[:, 0:NS], ALU.mult, ALU.add, [b_xcn, b_CA, sm_b], [sm_b])
                tt(cop[:, j, npr:npr + NS], gb_t[:, npr:npr + NS], sm_t[:, 0:NS], ALU.mult, [gb_b, sm_b], [b_cop])
        if last_group:
            pt, pb = psum.next()
            for j in range(4):
                tr(pt[0:2, j * 128:(j + 1) * 128], xc32[:, j, :], ident[:, :], [b_xc32], pb, last=(j == 3))
            o_t, o_b = ostage.next()
            cp(o_t[0:2, :], pt[0:2, :], [pb], [o_b], eng="act")
            out_dma(o_ccp[l, :, :], o_t[0:2, :], [o_b], dn(o_t))
            pt, pb = psum.next()
            for j in range(4):
                tr(pt[0:NS, j * 128:(j + 1) * 128], xcn[:, j, :], ident[:, :], [b_xcn], pb, last=(j == 3))
            o_t, o_b = ostage.next()
            cp(o_t[0:NS, :], pt[0:NS, :], [pb], [o_b], eng="act")
            out_dma(o_ccs[l, :, 1, :], o_t[0:NS, :], [o_b], dn(o_t))

        if mstop == "C":
            return
        pump_att(len(asteps))
        if ns:
            while steps:
                pump()
            pacc, paccb = sacc["p"]
            r_t, r_b = small.next()
            c.op("dve", lambda e, r_t=r_t: e.reciprocal(out=r_t[:, 0:64], in_=pacc[:, 64:128]), reads=[paccb], writes=[r_b])
            tt(qm[:, :, npr:npr + NS], pacc[:, 0:64].rearrange("p (h s) -> p h s", h=4),
               r_t[:, 0:64].rearrange("p (s h) -> p h s", h=4), ALU.mult, [paccb, r_b], [b_qm])
            psum.release((pacc, paccb))

        if mstop == "M":
            return
        claim([b_merged])
        wg_all = W["w_branch_gate"][l].rearrange("(k p) (b c) -> p k b c", p=128, b=4)
        wouts = [W["gmlp_w_out"][l], W["conv_b_w_out"][l], W["conv_c_w_out"][l], W["mem_w_out"][l]]
        opsrc = [(uA, b_uA), (bop, b_bop), (cop, b_cop), (qm, b_qm)]
        for p2 in range(4):
            gsl = []
            for half_ in range(2):
                gsl.append(wload([(lambda t, i=i: t[:, i * 2048:(i + 1) * 2048].rearrange("p (k c) -> p k c", k=8),
                                   wg_all[:, :, half_ * 2 + i, p2 * 256:(p2 + 1) * 256]) for i in range(2)]))
            ow_t, ow_b = wload([(lambda t, br=br: t[:, br * 1024:(br + 1) * 1024].rearrange("p (k c) -> p k c", k=4),
                                 wouts[br].rearrange("(k p) c -> p k c", p=128)[:, :, p2 * 256:(p2 + 1) * 256]) for br in range(4)])
            for dmi in range(2):
                dm = p2 * 2 + dmi
                for (c0, n, _) in mtiles:
                    prods = []
                    for br in (0, 2, 3, 1):
                        gw_t, gw_b = gsl[br // 2]
                        gv = gw_t[:, (br % 2) * 2048:(br % 2 + 1) * 2048].rearrange("p (k c) -> p k c", k=8)
                        ov = ow_t[:, br * 1024:(br + 1) * 1024].rearrange("p (k c) -> p k c", k=4)
                        pg_, pgb_ = psum.next()
                        for k in range(KD):
                            mm(pg_[:, 0:n], gv[:, k, dmi * 128:(dmi + 1) * 128], h[:, k, c0:c0 + n], k == 0, k == KD - 1, [gw_b, b_h], pgb_, k == KD - 1)
                        po_, pob_ = psum.next()
                        o_t_, o_b_ = opsrc[br]
                        for k in range(4):
                            mm(po_[:, 0:n], ov[:, k, dmi * 128:(dmi + 1) * 128], o_t_[:, k, c0:c0 + n], k == 0, k == 3, [ow_b, o_b_], pob_, k == 3)
                        s_t, s_b = tf.next()
                        actf(s_t[:, 0:n], pg_[:, 0:n], AF.Sigmoid, [pgb_, b_CA], [s_b], bias=colA(l, "bg", br * 8 + dm), scale=1.0)
                        tt(s_t[:, 0:n], s_t[:, 0:n], po_[:, 0:n], ALU.mult, [s_b, pob_], [s_b])
                        prods.append((s_t, s_b))
                    (a0, a0b), (a1, a1b), (a2, a2b), (a3, a3b) = prods
                    tt(a0[:, 0:n], a0[:, 0:n], a1[:, 0:n], ALU.add, [a0b, a1b], [a0b])
                    tt(a2[:, 0:n], a2[:, 0:n], a3[:, 0:n], ALU.add, [a2b, a3b], [a2b])
                    tt(merged[:, dm, c0:c0 + n], a0[:, 0:n], a2[:, 0:n], ALU.add, [a0b, a2b], [b_merged])
        if mstop == "G":
            return
        wo = W["w_o"][l].rearrange("(k p) c -> p k c", p=128)
        wos = [wload([(slot_view, wo[:, :, half * 512:(half + 1) * 512])]) for half in range(2)]
        for ti, (c0, n, _) in enumerate(mtiles):
            for dm in range(KD):
                wo_t, wo_b = wos[dm // 4]
                wov = slot_view(wo_t)
                dd = dm % 4
                pt, pb = psum.next()
                for k in range(KD):
                    mm(pt[:, 0:n], wov[:, k, dd * 128:(dd + 1) * 128], merged[:, k, c0:c0 + n], k == 0, k == KD - 1, [wo_b, b_merged], pb, k == KD - 1)
                tt(x[:, dm, c0:c0 + n], x[:, dm, c0:c0 + n], pt[:, 0:n], ALU.add, [b_x2[dm][ti], pb], [b_x2[dm][ti]])

    for gi in (range(len(GROUPS)) if run_groups is None else run_groups):
        t0, npr, ns, tiles, mtiles = group_info(gi)
        for ci in range(npr // 128):
            stg, b_stg = stage.next()
            c.dma("sp", lambda e, stg=stg, r0=t0 + ci * 128: e.dma_start(out=stg[:, :], in_=xp[r0:r0 + 128, :]),
                  dn(stg), writes=[b_stg])
            for k4 in range(2):
                pt, pb = psum.next()
                for kk in range(4):
                    k = k4 * 4 + kk
                    tr(pt[:, kk * 128:(kk + 1) * 128], stg[:, k * 128:(k + 1) * 128], ident[:], [b_stg], pb, last=(kk == 3))
                ti_ = [i for i, (c0_, n_, _) in enumerate(mtiles) if c0_ <= ci * 128 < c0_ + n_][0]
                cp(x[:, k4 * 4:(k4 + 1) * 4, ci * 128:(ci + 1) * 128], pt[:, :].rearrange("p (k t) -> p k t", k=4), [pb],
                   [b_x2[k][ti_] for k in range(k4 * 4, (k4 + 1) * 4)], eng="act" if k4 == 0 else "dve")
        if ns:
            stg, b_stg = stage.next()
            c.dma("sp", lambda e, stg=stg: e.dma_start(out=stg[0:NS, :], in_=xs[:, :]), dn(stg), writes=[b_stg])
            pt, pb = psum.next()
            for k in range(KD):
                tr(pt[:, k * NS:(k + 1) * NS], stg[0:NS, k * 128:(k + 1) * 128], ident[0:NS, 0:NS], [b_stg], pb, last=(k == KD - 1))
            cp(x[:, :, npr:npr + NS], pt[:, 0:KD * NS].rearrange("p (k t) -> p k t", k=KD), [pb],
               [b_x2[k][len(mtiles) - 1] for k in range(KD)], eng="act")

        nl = L
        for l in range(nl):
            if stop == "load":
                break
            if l == 0:
                kv_a(0, gi)
            ffn(l, mtiles, W["ffn1_w_gate_up"], W["ffn1_w_down"], "ffn1_norm",
                hooks=(lambda l=l, gi=gi: kv_b(l, gi), None, lambda l=l, gi=gi: kv_b1(l, gi)))
            if stop == "ffn1":
                break
            if stop is not None and stop.startswith("mix:"):
                mixing(l, gi, t0, npr, ns, tiles, mtiles, mstop=stop[4:])
                break
            mixing(l, gi, t0, npr, ns, tiles, mtiles)
            if stop == "mix":
                break
            ffn(l, mtiles, W["ffn2_w_gate_up"], W["ffn2_w_down"], "ffn2_norm",
                hooks=(None, (lambda l=l, gi=gi: kv_a(l + 1, gi)) if l + 1 < L else None))
            if stop == "layer0":
                break

        claim([b_yb])
        for half in range(2):
            def outfn(k, c0, n, half=half):
                return yb[:, k - half * 4, c0:c0 + n]
            for ti, (c0, n, _) in enumerate(mtiles):
                for k in range(KD):
                    actf(sq[:, k, 0:n], x[:, k, c0:c0 + n], AF.Square, [b_x2[k][ti]], [b_sq])
                pt, pb = psum.next()
                for k in range(KD):
                    mm(pt[:, 0:n], ones_b[:, :], sq[:, k, 0:n], k == 0, k == KD - 1, [b_ones, b_sq], pb, k == KD - 1)
                r_t, r_b = tf.next()
                actf(r_t[:, 0:n], pt[:, 0:n], AF.Sqrt, [pb, b_epsc], [r_b], bias=epsc[:, 0:1], scale=1.0 / D)
                c.op("dve", lambda e, r_t=r_t, n=n: e.reciprocal(out=r_t[:, 0:n], in_=r_t[:, 0:n]), reads=[r_b], writes=[r_b])
                for k in range(half * 4, half * 4 + 4):
                    stt(outfn(k, c0, n), x[:, k, c0:c0 + n], FN[:, k:k + 1], r_t[:, 0:n], ALU.mult, ALU.mult,
                        [b_x2[k][ti], r_b, b_FN], [b_yb])
            cols = [(ci * 128, 128, t0 + ci * 128, "p") for ci in range(npr // 128)] + ([(npr, NS, 0, "s")] if ns else [])
            for (c0, n, r0, kind) in cols:
                pt, pb = psum.next()
                for kk in range(4):
                    tr(pt[0:n, kk * 128:(kk + 1) * 128], yb[:, kk, c0:c0 + n], ident[:, :], [b_yb], pb, last=(kk == 3))
                o_t, o_b = ostage.next()
                cp(o_t[0:n, :], pt[0:n, :], [pb], [o_b], eng="act")
                dst = (y_p[r0:r0 + n, half * 512:(half + 1) * 512] if kind == "p" else y_s[:, half * 512:(half + 1) * 512])
                out_dma(dst, o_t[0:n, :], [o_b], dn(o_t), q="act")

    c.wait_all("sp", out_bufs)
    c.finalize()
    return nc


_PROG = {}


def _get_prog():
    if "nc" not in _PROG:
        _PROG["nc"] = build_program()
    return _PROG["nc"]


WEIGHT_NAMES = [
    "ffn1_norm", "ffn1_w_gate_up", "ffn1_w_down", "mix_norm", "w_in", "gmlp_ln_g", "gmlp_ln_b", "gmlp_w_s",
    "gmlp_b_s", "gmlp_w_out", "conv_b_w", "conv_b_bias", "conv_b_ln_g", "conv_b_ln_b", "conv_b_w_out",
    "conv_c_w", "conv_c_w_out", "mem_norm", "mem_w_k", "mem_w_v", "mem_w_out", "w_branch_gate",
    "b_branch_gate", "w_o", "ffn2_norm", "ffn2_w_gate_up", "ffn2_w_down", "final_norm",
]


def make_in_maps(inputs, cores):
    f = lambda a: np.ascontiguousarray(np.asarray(a, dtype=np.float32))
    wts = {n: f(inputs[n]) for n in WEIGHT_NAMES}
    maps = []
    for b in cores:
        m = dict(wts)
        m["xp"] = f(inputs["x_prompt"][b])
        m["xs"] = f(inputs["x_sample"][b * NS:(b + 1) * NS, 0, :])
        m["mem"] = f(inputs["mem_prompt"][b])
        m["scb"] = f(inputs["state_conv_b"][:, b * NS:(b + 1) * NS])
        m["scc"] = f(inputs["state_conv_c"][:, b * NS:(b + 1) * NS])
        m["ck"] = f(np.asarray(inputs["cache_mem_k"])[:, b * NS:(b + 1) * NS].reshape(L, NS, NMEM, 512))
        m["cv"] = f(np.asarray(inputs["cache_mem_v"])[:, b * NS:(b + 1) * NS].reshape(L, NS, NMEM, 512))
        maps.append(m)
    return maps


def assemble(results):
    n = len(results)
    y_prompt = np.stack([r["y_p"] for r in results], 0)
    y_sample = np.concatenate([r["y_s"] for r in results], 0)[:, None, :]
    mk = np.stack([r["o_mk"] for r in results], 1).reshape(L, n, NMEM, 4, 128)
    mv = np.stack([r["o_mv"] for r in results], 1).reshape(L, n, NMEM, 4, 128)
    cbp = np.stack([r["o_cbp"] for r in results], 1)
    cbs = np.concatenate([r["o_cbs"] for r in results], 1)
    ccp = np.stack([r["o_ccp"] for r in results], 1)
    ccs = np.concatenate([r["o_ccs"] for r in results], 1)
    gv = np.concatenate([r["o_gv"] for r in results], 1)[:, :, None, :]
    outs = (y_prompt, y_sample, mk, mv, cbp, cbs, ccp, ccs, gv)
    return tuple(np.ascontiguousarray(o, dtype=np.float32) for o in outs)


def kernel(**inputs):
    nc = _get_prog()
    in_maps = make_in_maps(inputs, list(range(N_CORES)))
    res = run_bass_kernel_spmd(nc, in_maps, core_ids=list(range(N_CORES)))
    return assemble(res.results)
```
